# Optimizing a Trainium2 kernel written in Bass

```python
import jax, jax.numpy as jnp
from jax import lax
import numpy as np

D_MODEL = 1024
BATCH = 16
SEQ = 2048
DEPTH = 1

EPS = 1e-6
A_GROUPS = 8
A_GROUP_DIM = D_MODEL // A_GROUPS
A_WIDTH = A_GROUPS * A_GROUP_DIM
CHUNK = 128
MLA_HEADS = 8
QK_NOPE_DIM = 128
QK_ROPE_DIM = 64
QK_HEAD_DIM = QK_NOPE_DIM + QK_ROPE_DIM
V_HEAD_DIM = D_MODEL // MLA_HEADS
Q_LORA_RANK = 256
KV_LORA_RANK = 128
ROPE_THETA = 10000.0
Q_BLOCK = 128
D_FF = 2816
CONV_WIDTH = 3

IN_DIM = 2 * A_WIDTH + Q_LORA_RANK + KV_LORA_RANK + QK_ROPE_DIM + 2 * D_MODEL
SPLIT_U = A_WIDTH
SPLIT_V = SPLIT_U + A_WIDTH
SPLIT_CQ = SPLIT_V + Q_LORA_RANK
SPLIT_CKV = SPLIT_CQ + KV_LORA_RANK
SPLIT_KR = SPLIT_CKV + QK_ROPE_DIM
SPLIT_GA = SPLIT_KR + D_MODEL

kernel_name = "hybrid_gmlp_mla_convffn"


def rms_norm(x, g):
    xf = x.astype(jnp.float32)
    xf = xf * lax.rsqrt(jnp.mean(jnp.square(xf), axis=-1, keepdims=True) + EPS)
    return xf.astype(x.dtype) * g


def layer_norm(x, g, b):
    xf = x.astype(jnp.float32)
    mu = jnp.mean(xf, axis=-1, keepdims=True)
    var = jnp.mean(jnp.square(xf - mu), axis=-1, keepdims=True)
    return ((xf - mu) * lax.rsqrt(var + EPS)).astype(x.dtype) * g + b


def rope_cos_sin(positions):
    inv_freq = 1.0 / (ROPE_THETA ** (jnp.arange(0, QK_ROPE_DIM, 2, dtype=jnp.float32) / QK_ROPE_DIM))
    ang = positions.astype(jnp.float32)[..., None] * inv_freq
    return jnp.cos(ang), jnp.sin(ang)


def apply_rope(x, cos, sin):
    x1, x2 = jnp.split(x.astype(jnp.float32), 2, axis=-1)
    return jnp.concatenate([x1 * cos - x2 * sin, x1 * sin + x2 * cos], axis=-1).astype(x.dtype)


def chunked_spatial_gating(u, v, v_g, v_b, w_s, b_s):
    B, S, _ = v.shape
    n_chunks = S // CHUNK
    v = layer_norm(v, v_g, v_b)
    vc = v.reshape(B, n_chunks, CHUNK, A_GROUPS, A_GROUP_DIM)
    causal = jnp.tril(jnp.ones((CHUNK, CHUNK), dtype=bool))
    w = jnp.where(causal[None], w_s, 0.0).astype(vc.dtype)
    mixed = jnp.einsum('gts,bnsgc->bntgc', w, vc) + b_s.T[None, None, :, :, None]
    return u * mixed.reshape(B, S, A_WIDTH)


def latent_attention(c_q, c_kv, k_rope, cos, sin, q_norm_g, w_uq, kv_norm_g, w_ukv):
    B, S, _ = c_q.shape
    q = (rms_norm(c_q, q_norm_g) @ w_uq).reshape(B, S, MLA_HEADS, QK_HEAD_DIM)
    q_nope, q_rope = jnp.split(q, [QK_NOPE_DIM], axis=-1)
    q_rope = apply_rope(q_rope, cos[:, :, None, :], sin[:, :, None, :])
    kv = (rms_norm(c_kv, kv_norm_g) @ w_ukv).reshape(B, S, MLA_HEADS, QK_NOPE_DIM + V_HEAD_DIM)
    k_nope, v = jnp.split(kv, [QK_NOPE_DIM], axis=-1)
    k_rope = apply_rope(k_rope, cos, sin)
    scale = QK_HEAD_DIM ** -0.5
    n_blocks = S // Q_BLOCK
    qn_blocks = q_nope.reshape(B, n_blocks, Q_BLOCK, MLA_HEADS, QK_NOPE_DIM).transpose(1, 0, 2, 3, 4)
    qr_blocks = q_rope.reshape(B, n_blocks, Q_BLOCK, MLA_HEADS, QK_ROPE_DIM).transpose(1, 0, 2, 3, 4)
    key_pos = jnp.arange(S)

    def one_block(args):
        qn, qr, i = args
        s = jnp.einsum('bqhd,bkhd->bhqk', qn, k_nope) + jnp.einsum('bqhr,bkr->bhqk', qr, k_rope)
        s = s.astype(jnp.float32) * scale
        q_pos = i * Q_BLOCK + jnp.arange(Q_BLOCK)
        s = jnp.where(key_pos[None, :] <= q_pos[:, None], s, -jnp.inf)
        p = jax.nn.softmax(s, axis=-1).astype(v.dtype)
        return jnp.einsum('bhqk,bkhd->bqhd', p, v)

    out = lax.map(one_block, (qn_blocks, qr_blocks, jnp.arange(n_blocks)))
    return out.transpose(1, 0, 2, 3, 4).reshape(B, S, MLA_HEADS * V_HEAD_DIM)


def causal_depthwise_conv(x, w, b):
    S = x.shape[1]
    xp = jnp.pad(x, ((0, 0), (CONV_WIDTH - 1, 0), (0, 0)))
    return b + sum(w[k] * xp[:, k:k + S] for k in range(CONV_WIDTH))


def conv_gated_ffn(h, w_up, conv_w, conv_b, w_down):
    up = causal_depthwise_conv(h @ w_up, conv_w, conv_b)
    gate, val = jnp.split(up, 2, axis=-1)
    return (jax.nn.silu(gate) * val) @ w_down


def setup_inputs(seed: int = 0) -> dict:
    key = jax.random.key(seed)
    ks = jax.random.split(key, 20)
    f32 = jnp.float32

    def nrm(k, shape, fan_in):
        return jax.random.normal(k, shape, f32) * (fan_in ** -0.5)

    def gain(k, shape):
        return 1.0 + 0.02 * jax.random.normal(k, shape, f32)

    x = jax.random.normal(ks[0], (BATCH, SEQ, D_MODEL), f32)
    offset = jax.random.randint(ks[1], (BATCH, 1), 0, 1024, dtype=jnp.int32)
    positions = jnp.arange(SEQ, dtype=jnp.int32)[None, :] + offset
    return {
        "x": x,
        "positions": positions,
        "mix_norm": gain(ks[2], (DEPTH, D_MODEL)),
        "w_in": nrm(ks[3], (DEPTH, D_MODEL, IN_DIM), D_MODEL),
        "a_v_norm_g": gain(ks[4], (DEPTH, A_WIDTH)),
        "a_v_norm_b": 0.02 * jax.random.normal(ks[5], (DEPTH, A_WIDTH), f32),
        "a_spatial_w": nrm(ks[6], (DEPTH, A_GROUPS, CHUNK, CHUNK), CHUNK),
        "a_spatial_b": gain(ks[7], (DEPTH, A_GROUPS, CHUNK)),
        "q_a_norm": gain(ks[8], (DEPTH, Q_LORA_RANK)),
        "w_uq": nrm(ks[9], (DEPTH, Q_LORA_RANK, MLA_HEADS * QK_HEAD_DIM), Q_LORA_RANK),
        "kv_a_norm": gain(ks[10], (DEPTH, KV_LORA_RANK)),
        "w_ukv": nrm(ks[11], (DEPTH, KV_LORA_RANK, MLA_HEADS * (QK_NOPE_DIM + V_HEAD_DIM)), KV_LORA_RANK),
        "w_out": nrm(ks[12], (DEPTH, D_MODEL, D_MODEL), D_MODEL),
        "ffn_norm": gain(ks[13], (DEPTH, D_MODEL)),
        "w_up": nrm(ks[14], (DEPTH, D_MODEL, 2 * D_FF), D_MODEL),
        "conv_w": nrm(ks[15], (DEPTH, CONV_WIDTH, 2 * D_FF), CONV_WIDTH),
        "conv_b": 0.02 * jax.random.normal(ks[16], (DEPTH, 2 * D_FF), f32),
        "w_down": nrm(ks[17], (DEPTH, D_FF, D_MODEL), D_FF),
        "final_norm": gain(ks[18], (D_MODEL,)),
    }


def reference(x, positions, mix_norm, w_in, a_v_norm_g, a_v_norm_b, a_spatial_w, a_spatial_b,
              q_a_norm, w_uq, kv_a_norm, w_ukv, w_out, ffn_norm, w_up, conv_w, conv_b, w_down,
              final_norm):
    cos, sin = rope_cos_sin(positions)
    for l in range(DEPTH):
        h = rms_norm(x, mix_norm[l])
        z = h @ w_in[l]
        u_a, v_a, c_q, c_kv, k_rope, g_a, g_b = jnp.split(
            z, [SPLIT_U, SPLIT_V, SPLIT_CQ, SPLIT_CKV, SPLIT_KR, SPLIT_GA], axis=-1)
        y_a = chunked_spatial_gating(jax.nn.gelu(u_a), jax.nn.gelu(v_a), a_v_norm_g[l], a_v_norm_b[l],
                                     a_spatial_w[l], a_spatial_b[l])
        y_b = latent_attention(c_q, c_kv, k_rope, cos, sin, q_a_norm[l], w_uq[l],
                               kv_a_norm[l], w_ukv[l])
        merged = jax.nn.sigmoid(g_a) * y_a + jax.nn.sigmoid(g_b) * y_b
        x = x + merged @ w_out[l]
        x = x + conv_gated_ffn(rms_norm(x, ffn_norm[l]), w_up[l], conv_w[l], conv_b[l], w_down[l])
    return rms_norm(x, final_norm)
```

```python
import contextlib
import numpy as np
import concourse.bass as bass
import concourse.mybir as mybir
from concourse.bass_utils import run_bass_kernel_spmd

F32 = mybir.dt.float32
BF16 = mybir.dt.bfloat16
I32 = mybir.dt.int32
AF = mybir.ActivationFunctionType
ALU = mybir.AluOpType

S_LEN = 2048
D = 1024
GT = 512
NSLOT = 29
RING = 3
SCALE = 192.0 ** -0.5
EPS = 1e-6
MAGIC = 12582912.0
TWO_PI = float(2.0 * np.pi)
NCOL = 200
C_EPS, C_INVF, C_SGN, C_G1, C_G2, C_QG, C_KG, C_CB, C_CW = 0, 1, 2, 4, 12, 20, 22, 23, 67


class Sched:
    def __init__(self, nc):
        self.nc = nc
        self.ops = []
        self.last_w = {}
        self.readers = {}

    def op(self, eng, fn, reads=(), writes=(), dma_key=None, extra=()):
        idx = len(self.ops)
        deps = set(extra)
        for r in reads:
            if r in self.last_w:
                deps.add(self.last_w[r])
        for w in writes:
            if w in self.last_w:
                deps.add(self.last_w[w])
            for rd in self.readers.get(w, {}).values():
                deps.add(rd)
        deps.discard(idx)
        rk = eng if dma_key is None else 'd:' + str(dma_key)
        for r in reads:
            self.readers.setdefault(r, {})[rk] = idx
        for w in writes:
            self.last_w[w] = idx
            self.readers[w] = {}
        self.ops.append(dict(eng=eng, fn=fn, deps=deps, dma_key=dma_key))
        return idx

    def emit(self, final_wait_eng='sp'):
        nc = self.nc
        ops = self.ops
        engs = ['pe', 'act', 'dve', 'pool', 'sp']
        need = [False] * len(ops)
        for i, o in enumerate(ops):
            for d in o['deps']:
                p = ops[d]
                if p['dma_key'] is None and p['eng'] == 'pe' and o['eng'] == 'pe':
                    continue
                need[d] = True
        dma_keys = []
        for o in ops:
            if o['dma_key'] is not None and o['dma_key'] not in dma_keys:
                dma_keys.append(o['dma_key'])
        cnt = {e: 0 for e in engs}
        dcnt = {k: 0 for k in dma_keys}
        for i, o in enumerate(ops):
            if o['dma_key'] is not None:
                dcnt[o['dma_key']] += 16
                o['ms'] = ('d:' + str(o['dma_key']), dcnt[o['dma_key']])
            elif need[i]:
                cnt[o['eng']] += 1
                o['ms'] = ('e:' + o['eng'], cnt[o['eng']])
            else:
                o['ms'] = None
        sem_names = ['e:' + e for e in engs if cnt[e] > 0] + ['d:' + str(k) for k in dma_keys]
        self.n_sems = len(sem_names)
        self.counts = dict(cnt)
        with contextlib.ExitStack() as st:
            sems = {}
            for j, n in enumerate(sem_names):
                sems[n] = st.enter_context(nc.semaphore('s%d' % j))
            block = st.enter_context(nc.Block())
            per_eng = {e: [i for i, o in enumerate(ops) if o['eng'] == e] for e in engs}
            final = [('d:' + str(k), dcnt[k]) for k in dma_keys]

            def make(ename):
                def body(eng):
                    waited = {}
                    for i in per_eng[ename]:
                        o = ops[i]
                        for d in sorted(o['deps']):
                            p = ops[d]
                            if p['dma_key'] is None and p['eng'] == 'pe' and ename == 'pe':
                                continue
                            sn, val = p['ms']
                            if waited.get(sn, 0) >= val:
                                continue
                            eng.wait_ge(sems[sn], val)
                            waited[sn] = val
                        ins = o['fn'](eng)
                        if o['ms'] is not None:
                            sn, val = o['ms']
                            ins.then_inc(sems[sn], 16 if sn.startswith('d:') else 1)
                    if ename == final_wait_eng:
                        for sn, val in final:
                            if waited.get(sn, 0) < val:
                                eng.wait_ge(sems[sn], val)
                return body

            block.sync(make('sp'))
            if per_eng['pe']:
                block.tensor(make('pe'))
            if per_eng['act']:
                block.scalar(make('act'))
            if per_eng['dve']:
                block.vector(make('dve'))
            if per_eng['pool']:
                block.gpsimd(make('pool'))


def build_program(n_seq=2, n_grp=4, dbg=None):
    nc = bass.Bass("TRN2", target_bir_lowering=False)
    x_d = nc.dram_tensor("x", [n_seq, S_LEN, D], F32, kind="ExternalInput").ap()
    pos_d = nc.dram_tensor("pos", [n_seq, S_LEN], I32, kind="ExternalInput").ap()
    wts_d = nc.dram_tensor("wts", [NSLOT, 128, 4096], F32, kind="ExternalInput").ap()
    wsm_d = nc.dram_tensor("wsm", [128, 3072], F32, kind="ExternalInput").ap()
    cols_d = nc.dram_tensor("cols", [128, NCOL], F32, kind="ExternalInput").ap()
    rows_d = nc.dram_tensor("rows", [3, D], F32, kind="ExternalInput").ap()
    bsr_d = nc.dram_tensor("bsr", [1, D], F32, kind="ExternalInput").ap()
    out_d = nc.dram_tensor("out", [n_seq, S_LEN, D], F32, kind="ExternalOutput").ap()
    wtb_d = nc.dram_tensor("wtb", [NSLOT, 128, 4096], BF16, kind="Internal").ap()

    S = Sched(nc)
    _n = [0]

    def sb(shape, dt, name=None):
        _n[0] += 1
        return nc.alloc_sbuf_tensor("sb_" + (name or ("t%d" % _n[0])), shape, dt).ap()

    def mm(out, lhsT, rhs, start, stop, reads, writes):
        S.op('pe', lambda e: e.matmul(out, lhsT=lhsT, rhs=rhs, start=start, stop=stop,
                                      skip_group_check=True), reads=reads, writes=writes)

    def tr(out, in_, reads, writes):
        S.op('pe', lambda e: e.transpose(out, in_, ident), reads=list(reads) + ['ident'], writes=writes)

    def act(out, in_, func, reads, writes, scale=None, bias=None, accum=None):
        def f(e):
            kw = {}
            if scale is not None:
                kw['scale'] = scale
            if bias is not None:
                kw['bias'] = bias
            if accum is not None:
                kw['accum_out'] = accum
            return e.activation(out=out, in_=in_, func=func, **kw)
        S.op('act', f, reads=reads, writes=writes)

    def tt(eng, out, in0, in1, op, reads, writes):
        S.op(eng, lambda e: e.tensor_tensor(out=out, in0=in0, in1=in1, op=op), reads=reads, writes=writes)

    def ts(eng, out, in0, s1, op0, reads, writes, s2=None, op1=None):
        def f(e):
            if op1 is None:
                return e.tensor_scalar(out=out, in0=in0, scalar1=s1, scalar2=None, op0=op0)
            return e.tensor_scalar(out=out, in0=in0, scalar1=s1, scalar2=s2, op0=op0, op1=op1)
        S.op(eng, f, reads=reads, writes=writes)

    def stt(out, in0, scalar, in1, op0, op1, reads, writes):
        S.op('dve', lambda e: e.scalar_tensor_tensor(out=out, in0=in0, scalar=scalar, in1=in1, op0=op0, op1=op1),
             reads=reads, writes=writes)

    def cp(eng, out, in_, reads, writes):
        S.op(eng, lambda e: e.tensor_copy(out=out, in_=in_), reads=reads, writes=writes)

    sp_hist = []
    pool_hist = []
    MAX_OUTSTANDING = 6

    def dma(eng, out, in_, key, reads, writes):
        extra = ()
        if eng == 'sp' and len(sp_hist) >= MAX_OUTSTANDING:
            extra = (sp_hist[-MAX_OUTSTANDING],)
        if eng == 'pool' and len(pool_hist) >= 2:
            extra = (pool_hist[-2],)
        idx = S.op(eng, lambda e: e.dma_start(out=out, in_=in_), reads=reads, writes=writes, dma_key=key, extra=extra)
        if eng == 'sp':
            sp_hist.append(idx)
        if eng == 'pool':
            pool_hist.append(idx)

    def dump(name, ap, reads, shape, dt=F32):
        if dbg is None or name not in dbg:
            return
        d = nc.dram_tensor("dbg_" + name, list(shape), dt, kind="ExternalOutput").ap()
        dma('sp', d, ap, 'dbg_' + name, reads, [])

    banks = [nc.alloc_psum_tensor("bank%d" % b, [128, 512], F32).ap() for b in range(8)]
    banks_bf = [bk.bitcast(BF16) for bk in banks]

    def PB(b):
        return 'ps%d' % b

    colsT = sb([128, NCOL], F32, "cols")
    lng = sb([128, D], F32, "lng")
    lnb = sb([128, D], F32, "lnb")
    fng = sb([128, D], F32, "fng")
    ident = sb([128, 128], BF16, "ident")
    ones = sb([128, 128], BF16, "ones")
    tri = sb([128, 128], BF16, "tri")
    ones64 = sb([64, 128], BF16, "ones64")
    bsT = sb([64, D], BF16, "bsT")
    negh = sb([128, 8], F32, "negh")
    wsm = sb([128, 3, 8, 128], BF16, "wsm")
    ring = [sb([128, 4096], BF16, "ring%d" % r) for r in range(RING)]
    ckvnT = sb([128, S_LEN], BF16, "ckvnT")
    Vc = sb([128, 16, 128], BF16, "Vc")
    krT = sb([128, S_LEN], BF16, "krT")
    krB = sb([128, S_LEN], BF16, "krB")
    x1 = sb([128, 4, D], F32, "x1")
    carry2 = [sb([128, 44, 2], F32, "carry%d" % i) for i in range(2)]
    xs = [sb([128, D], F32, "xs%d" % i) for i in range(2)]
    junk = sb([128, D], BF16, "junk")
    hT = sb([128, 8, GT], BF16, "hT")
    ss = sb([128, 32], F32, "ss")
    bufA = sb([128, 3, 8, GT], BF16, "bufA")
    uT = bufA[:, 0]
    gbT = bufA[:, 1]
    qpT = bufA[:, 2]
    aT = bufA.rearrange("p a k t -> p (a k) t")
    gaT = [sb([128, GT], BF16, "gaT%d" % i) for i in range(2)]
    v32 = [sb([128, D], F32, "v32_%d" % i) for i in range(2)]
    vn = [sb([128, D], BF16, "vn%d" % i) for i in range(4)]
    hbf = vn
    stats = sb([128, 2, 6], F32, "stats")
    mv = sb([128, 4], F32, "mv")
    bufB = sb([128, 5632], F32, "bufB")
    cq32 = bufB[:, 0:1024].rearrange("p (a t) -> p a t", a=2)
    rstdq = bufB[:, 1024:2048].rearrange("p (a t) -> p a t", a=2)
    ckv32 = bufB[:, 2048:2560]
    rt = [bufB[:, 2560:3072], bufB[:, 3072:3584]]
    rcp = [bufB[:, 3584:4096], bufB[:, 4096:4608]]
    mb32 = [bufB[:, 4608:5120], bufB[:, 5120:5632]]
    U = [[bufB[:, (a * 2 + b) * 514:(a * 2 + b + 1) * 514] for b in range(2)] for a in range(2)]
    yb = [[bufB[:, 2056 + (a * 2 + b) * 512:2056 + (a * 2 + b + 1) * 512] for b in range(2)] for a in range(2)]
    sg = [bufB[:, 4104:4616], bufB[:, 4616:5128]]
    sq = sb([128, 3, GT], BF16, "sq")
    cqnT = sb([128, 2, GT], BF16, "cqnT")
    CS = [[sb([128, GT], F32, "C%d" % i), sb([128, GT], F32, "S%d" % i)] for i in range(2)]
    posi = sb([128, GT], I32, "posi")
    ang = sb([128, GT], F32, "ang")
    angt = sb([128, GT], F32, "angt")
    qn = [sb([128, GT], BF16, "qn%d" % i) for i in range(2)]
    qrT = sb([128, 4, GT], BF16, "qrT")
    PT = [sb([128, GT], BF16, "PT%d" % i) for i in range(4)]
    oln = [sb([128, GT], BF16, "oln%d" % i) for i in range(2)]
    ot = v32
    bstage = v32[0][0:64, :]
    identf = v32[1][:, 0:128]
    trif = v32[1][:, 128:256]
    bstmp = vn[0][0:64, :]

    def col(c, n=1):
        return colsT[:, c:c + n]

    cast_done = [0]
    CAST_AHEAD = 6

    def cast_upto(n):
        while cast_done[0] < min(n, NSLOT):
            s_ = cast_done[0]
            dma('pool', wtb_d[s_], wts_d[s_], 'cast%d' % s_, [], ['wtb%d' % s_])
            cast_done[0] += 1
    dma('sp', colsT, cols_d, 'cols', [], ['cols'])
    dma('sp', lng, rows_d[0:1, :].partition_broadcast(128), 'lng', [], ['lng'])
    dma('sp', lnb, rows_d[1:2, :].partition_broadcast(128), 'lnb', [], ['lnb'])
    dma('sp', fng, rows_d[2:3, :].partition_broadcast(128), 'fng', [], ['fng'])
    dma('sp', bstage[0:1, :], bsr_d, 'bs0', [], ['v32_0h0'])
    dma('sp', bstage[32:33, :], bsr_d, 'bs32', [], ['v32_0h1'])
    dma('pool', wsm.rearrange("p a h l -> p (a h l)"), wsm_d, 'wsm', [], ['wsm'])
    S.op('pool', lambda e: e.memset(identf, 1.0), writes=['v32_1h0'])
    S.op('pool', lambda e: e.affine_select(out=identf, in_=identf, pattern=[[-1, 128]], compare_op=ALU.is_equal,
                                           fill=0.0, base=0, channel_multiplier=1), reads=['v32_1h0'], writes=['v32_1h0'])
    cp('pool', ident, identf, ['v32_1h0'], ['ident'])
    S.op('pool', lambda e: e.memset(ones, 1.0), writes=['ones'])
    S.op('pool', lambda e: e.memset(negh, -0.5), writes=['negh'])
    S.op('pool', lambda e: e.memset(trif, 0.0), writes=['v32_1h0'])
    S.op('pool', lambda e: e.affine_select(out=trif, in_=trif, pattern=[[1, 128]], compare_op=ALU.is_ge,
                                           fill=-30000.0, base=0, channel_multiplier=-1), reads=['v32_1h0'], writes=['v32_1h0'])
    cp('pool', tri, trif, ['v32_1h0'], ['tri'])
    spv = wsm[:, 2]
    S.op('pool', lambda e: e.affine_select(out=spv, in_=spv, pattern=[[0, 8], [1, 128]], compare_op=ALU.is_ge,
                                           fill=0.0, base=0, channel_multiplier=-1), reads=['wsm'], writes=['wsm'])
    S.op('pool', lambda e: e.memset(ones64, 0.0), writes=['ones64'])
    S.op('pool', lambda e: e.memset(ones64[0:1, :], 1.0), reads=['ones64'], writes=['ones64'])
    S.op('pool', lambda e: e.memset(ones64[32:33, :], 1.0), reads=['ones64'], writes=['ones64'])
    S.op('pool', lambda e: e.memset(bsT, 0.0), writes=['bsT'])
    cp('dve', bsT[0:1, :], bstage[0:1, :], ['v32_0h0', 'bsT'], ['bsT'])
    cp('dve', bstmp[32:33, :], bstage[32:33, :], ['v32_0h1'], ['vn0'])
    tt('dve', bsT[32:33, :], bstage[32:33, :], bstmp[32:33, :], ALU.subtract, ['v32_0h1', 'vn0', 'bsT'], ['bsT'])
    S.op('pool', lambda e: e.memset(krT[64:128, :], 0.0), writes=['krT'])
    S.op('pool', lambda e: e.memset(krB[0:64, :], 0.0), writes=['krB'])
    cast_upto(CAST_AHEAD)

    ring_pos = [0]

    def next_slot(slot_id):
        cast_upto(slot_id + 1 + CAST_AHEAD)
        r = ring_pos[0] % RING
        ring_pos[0] += 1
        dma('sp', ring[r], wtb_d[slot_id], 'ring%d' % r, ['wtb%d' % slot_id], ['ring%d' % r])
        return ring[r], 'ring%d' % r

    bank_rr = [0]

    def nb(allowed):
        b = allowed[bank_rr[0] % len(allowed)]
        bank_rr[0] += 1
        return b

    def rsqrt_small(dst, src, n, key):
        tt('pool', dst, src, negh[:, 0:n], ALU.pow, [key + '_ms', 'negh'], [key + '_rs'])

    def rope_tables(seq, g, buf):
        t0 = g * GT
        Ct, St = CS[buf]
        kC, kS = 'C%d' % buf, 'S%d' % buf
        dma('sp', posi, pos_d[seq:seq + 1, t0:t0 + GT].partition_broadcast(128), 'pos', [], ['posi'])
        cp('dve', ang, posi, ['posi'], ['ang'])
        ts('dve', ang, ang, col(C_INVF), ALU.mult, ['ang', 'cols'], ['ang'])
        C1 = 6.28125
        C2 = float(2.0 * np.pi - 6.28125)

        def reduce_into(dst, dkey):
            ts('dve', angt, ang, 1.0 / TWO_PI, ALU.mult, ['ang'], ['angt'], s2=MAGIC, op1=ALU.add)
            ts('dve', angt, angt, MAGIC, ALU.subtract, ['angt'], ['angt'])
            stt(dst, angt, -C1, ang, ALU.mult, ALU.add, ['angt', 'ang'], [dkey])
            stt(dst, angt, -C2, dst, ALU.mult, ALU.add, ['angt', dkey], [dkey])
            ts('dve', dst, dst, float(np.pi), ALU.min, [dkey], [dkey], s2=-float(np.pi), op1=ALU.max)
        reduce_into(St, kS)
        act(St, St, AF.Sin, [kS, 'cols'], [kS], scale=col(C_SGN))
        ts('dve', ang, ang, float(np.pi / 2), ALU.add, ['ang'], ['ang'])
        reduce_into(Ct, kC)
        act(Ct, Ct, AF.Sin, [kC], [kC])

    def rms_pre(i, src, skey, ssbase, tag):
        c_s = ss[:, ssbase + i:ssbase + i + 1]
        c_r = ss[:, ssbase + 4 + i:ssbase + 5 + i]
        act(junk, src, AF.Square, [skey], [tag + 'ss%d' % i, 'junk'], accum=c_s)
        ts('dve', c_s, c_s, 1.0 / D, ALU.mult, [tag + 'ss%d' % i], [tag + 'ss%d_ms' % i], s2=EPS, op1=ALU.add)
        rsqrt_small(c_r, c_s, 1, tag + 'ss%d' % i)
        act(hbf[i], src, AF.Copy, [skey, tag + 'ss%d_rs' % i], ['vn%d' % i], scale=c_r)

    def rms_pe(i, gcol):
        b = nb([0, 1])
        for k in range(8):
            tr(banks_bf[b][:, k * 128:(k + 1) * 128], hbf[i][:, k * 128:(k + 1) * 128], ['vn%d' % i], [PB(b)])
        tt('dve', hT[:, :, i * 128:(i + 1) * 128],
           banks_bf[b].rearrange("p (k t) -> p k t", k=8),
           col(gcol, 8).unsqueeze(2).to_broadcast([128, 8, 128]), ALU.mult,
           [PB(b), 'cols'], ['hT'])

    def m1_load(seq, g, part):
        t0 = g * GT
        for i in (0, 1) if part == 0 else (2, 3):
            dma('sp', xs[i % 2], x_d[seq, t0 + i * 128:t0 + (i + 1) * 128, :], 'xs%d' % (i % 2), [], ['xs%d' % (i % 2)])

    def m1_pre(seq, g, part):
        for i in (0, 1) if part == 0 else (2, 3):
            rms_pre(i, xs[i % 2], 'xs%d' % (i % 2), 0, 'n1')

    def m1_pe():
        for i in range(4):
            rms_pe(i, C_G1)

    first = True
    pending_final = [None]
    pending_store = [None]
    for seq in range(n_seq):
        for g in range(n_grp):
            t0 = g * GT
            gi = seq * n_grp + g
            buf = gi % 2
            Ct, St = CS[buf]
            kC, kS = 'C%d' % buf, 'S%d' % buf
            if first:
                rope_tables(seq, g, buf)
                first = False
            if gi == 0:
                m1_load(seq, g, 0)
                m1_pre(seq, g, 0)
                m1_load(seq, g, 1)
                m1_pre(seq, g, 1)
                m1_pe()
            if g == 0 and seq == 0:
                dump('hT', hT, ['hT'], [128, 8, GT], BF16)

            s0, k0 = next_slot(0)
            s1, k1 = next_slot(1)
            s0v = s0.rearrange("p (k c) -> p k c", k=8)
            s1v = s1.rearrange("p (k c) -> p k c", k=8)
            for i in range(4):
                vb = v32[i % 2]
                vkey = 'v32_%d' % (i % 2)
                for half, (sv, sk) in enumerate(((s0v, k0), (s1v, k1))):
                    b = nb([2, 3, 4, 5])
                    for k in range(8):
                        mm(banks[b], hT[:, k, i * 128:(i + 1) * 128], sv[:, k, :], k == 0, k == 7, ['hT', sk], [PB(b)])
                    act(vb[:, half * 512:(half + 1) * 512], banks[b], AF.Gelu_apprx_tanh, [PB(b)], [vkey + 'h%d' % half])
                for half in range(2):
                    S.op('dve', (lambda vb=vb, half=half: (lambda e: e.bn_stats(out=stats[:, half, :], in_=vb[:, half * 512:(half + 1) * 512])))(),
                         reads=[vkey + 'h%d' % half], writes=['stats%d' % half])
                S.op('dve', lambda e: e.bn_aggr(out=mv[:, 0:2], in_=stats.rearrange("p a b -> p (a b)")),
                     reads=['stats0', 'stats1'], writes=['mv'])
                ts('dve', mv[:, 2:3], mv[:, 1:2], EPS, ALU.add, ['mv'], ['mv_ms'])
                rsqrt_small(mv[:, 3:4], mv[:, 2:3], 1, 'mv')
                ts('dve', vb, vb, mv[:, 0:1], ALU.subtract, [vkey + 'h0', vkey + 'h1', 'mv', 'mv_rs'], [vkey + 'h0', vkey + 'h1'],
                   s2=mv[:, 3:4], op1=ALU.mult)
                tt('pool', vb, vb, lng, ALU.mult, [vkey + 'h0', vkey + 'h1', 'lng'], [vkey + 'h0', vkey + 'h1'])
                tt('pool', vn[i], vb, lnb, ALU.add, [vkey + 'h0', vkey + 'h1', 'lnb'], ['vn%d' % i])
                if g == 0 and seq == 0 and i == 0:
                    dump('vn', vn[0], ['vn0'], [128, D], BF16)

            s2, k2 = next_slot(2)
            s2v = s2.rearrange("p (k c) -> p k c", k=8)
            for a in range(2):
                b = nb([2, 3, 4, 5])
                for k in range(8):
                    mm(banks[b], s2v[:, k, a * 128:(a + 1) * 128], hT[:, k, :], k == 0, k == 7, ['hT', k2], [PB(b)])
                act(cq32[:, a, :], banks[b], AF.Copy, [PB(b)], ['cq32_%d' % a])
                act(sq[:, a, :], banks[b], AF.Square, [PB(b)], ['sq%d' % a])
            b = nb([2, 3, 4, 5])
            for k in range(8):
                mm(banks[b], s2v[:, k, 256:384], hT[:, k, :], k == 0, k == 7, ['hT', k2], [PB(b)])
            act(ckv32, banks[b], AF.Copy, [PB(b)], ['ckv32'])
            act(sq[:, 2, :], banks[b], AF.Square, [PB(b)], ['sq2'])
            bkr = nb([2, 3, 4, 5])
            for k in range(8):
                mm(banks[bkr][0:64, :], s2v[:, k, 384:448], hT[:, k, :], k == 0, k == 7, ['hT', k2], [PB(bkr)])
            bkp = nb([2, 3, 4, 5])
            for k in range(8):
                mm(banks[bkp][0:64, :], s2v[:, k, 448:512], hT[:, k, :], k == 0, k == 7, ['hT', k2], [PB(bkp)])
            tt('dve', rt[0][0:64, :], banks[bkr][0:64, :], Ct[0:64, :], ALU.mult, [PB(bkr), kC], ['rt0'])
            tt('dve', rt[1][0:64, :], banks[bkp][0:64, :], St[0:64, :], ALU.mult, [PB(bkp), kS], ['rt1'])
            tt('pool', krT[0:64, t0:t0 + GT], rt[0][0:64, :], rt[1][0:64, :], ALU.add, ['rt0', 'rt1'], ['krT'])
            cp('dve', krB[64:128, t0:t0 + GT], krT[0:64, t0:t0 + GT], ['krT'], ['krB'])
            def emit_rq(t0=t0):
                b = nb([2, 3, 4, 5])
                for a in range(2):
                    mm(banks[b], ones, sq[:, a, :], a == 0, a == 1, ['ones', 'sq%d' % a], [PB(b)])
                act(rstdq[:, 0, :], banks[b], AF.Ln, [PB(b), 'cols'], ['rq0_ms'], scale=1.0 / 256, bias=col(C_EPS))
                act(rstdq[:, 0, :], rstdq[:, 0, :], AF.Exp, ['rq0_ms'], ['rq0_rs'], scale=-0.5)
                for a in range(2):
                    stt(cqnT[:, a, :], cq32[:, a, :], col(C_QG + a), rstdq[:, 0, :], ALU.mult, ALU.mult,
                        ['cq32_%d' % a, 'rq0_rs', 'cols'], ['cqnT%d' % a])
                b = nb([2, 3, 4, 5])
                mm(banks[b], ones, sq[:, 2, :], True, True, ['ones', 'sq2'], [PB(b)])
                act(rstdq[:, 1, :], banks[b], AF.Ln, [PB(b), 'cols'], ['rq1_ms'], scale=1.0 / 128, bias=col(C_EPS))
                act(rstdq[:, 1, :], rstdq[:, 1, :], AF.Exp, ['rq1_ms'], ['rq1_rs'], scale=-0.5)
                stt(ckvnT[:, t0:t0 + GT], ckv32, col(C_KG), rstdq[:, 1, :], ALU.mult, ALU.mult,
                    ['ckv32', 'rq1_rs', 'cols'], ['ckvnT'])
            def emit_vt(t0=t0, g=g):
                b = nb([0, 1])
                for i in range(4):
                    tr(banks_bf[b][:, i * 128:(i + 1) * 128], ckvnT[:, t0 + i * 128:t0 + (i + 1) * 128], ['ckvnT'], [PB(b)])
                cp('dve', Vc[:, 4 * g:4 * g + 4, :], banks_bf[b][:, 0:512].rearrange("p (i l) -> p i l", i=4), [PB(b)], ['Vc'])
            for sidx in range(3, 9):
                if sidx == 4:
                    emit_rq()
                if pending_final[0] is not None and sidx < 7:
                    pending_final[0](sidx - 3)
                    if sidx == 6:
                        pending_final[0] = None
                sl, sk = next_slot(sidx)
                slv = sl.rearrange("p (k c) -> p k c", k=8)
                for cc in range(4):
                    c = ((sidx - 3) % 2) * 4 + cc
                    b = nb([2, 3, 4, 5])
                    for k in range(8):
                        mm(banks[b], slv[:, k, cc * 128:(cc + 1) * 128], hT[:, k, :], k == 0, k == 7, ['hT', sk], [PB(b)])
                    if sidx < 5:
                        act(uT[:, c, :], banks[b], AF.Gelu_apprx_tanh, [PB(b)], ['uT%d' % c])
                    elif sidx < 7:
                        act(gaT[c % 2], banks[b], AF.Sigmoid, [PB(b)], ['gaT%d' % (c % 2)])
                        tt('pool', uT[:, c, :], uT[:, c, :], gaT[c % 2], ALU.mult, ['uT%d' % c, 'gaT%d' % (c % 2)], ['uT%d' % c])
                    else:
                        act(gbT[:, c, :], banks[b], AF.Sigmoid, [PB(b)], ['gbT%d' % c])

            emit_vt()
            if g == 0 and seq == 0:
                dump('ckvnT', ckvnT[:, 0:GT], ['ckvnT'], [128, GT], BF16)
                dump('krT', krT[:, 0:GT], ['krT'], [128, GT], BF16)

            for i in range(4):
                for hb2 in range(2):
                    b = nb([6, 7])
                    for gg in range(4):
                        gr = hb2 * 4 + gg
                        mm(banks[b][:, gg * 128:(gg + 1) * 128], vn[i][:, gr * 128:(gr + 1) * 128], wsm[:, 2, gr, :],
                           True, False, ['vn%d' % i, 'wsm'], [PB(b)])
                        mm(banks[b][:, gg * 128:(gg + 1) * 128], ones64, bsT[:, gr * 128:(gr + 1) * 128],
                           False, True, ['ones64', 'bsT'], [PB(b)])
                    tt('dve', uT[:, hb2 * 4:hb2 * 4 + 4, i * 128:(i + 1) * 128],
                       banks[b].rearrange("p (g t) -> p g t", g=4),
                       uT[:, hb2 * 4:hb2 * 4 + 4, i * 128:(i + 1) * 128], ALU.mult,
                       [PB(b)] + ['uT%d' % (hb2 * 4 + q) for q in range(4)], ['uT%d' % (hb2 * 4 + q) for q in range(4)])
            if g == 0 and seq == 0:
                dump('maT', uT, ['uT%d' % q for q in range(8)], [128, 8, GT], BF16)

            s9, k9 = next_slot(9)
            s9v = s9.rearrange("p (a c) -> p a c", a=2)
            def q_a(h):
                b = nb([2, 3, 4, 5])
                for a in range(2):
                    mm(banks[b], s9v[:, a, h * 128:(h + 1) * 128], cqnT[:, a, :], a == 0, a == 1, ['cqnT%d' % a, k9], [PB(b)])
                act(qn[h % 2], banks[b], AF.Copy, [PB(b)], ['qn%d' % (h % 2)])

            def q_b(h):
                b = nb([2, 3, 4, 5])
                mm(banks[b], wsm[:, 0, h, :], qn[h % 2], True, True, ['wsm', 'qn%d' % (h % 2)], [PB(b)])
                cp('dve', qpT[:, h, :], banks[b], [PB(b)], ['qpT%d' % h])
            q_a(0)
            for h in range(8):
                if h + 1 < 8:
                    q_a(h + 1)
                q_b(h)
            for pr in range(4):
                b1 = nb([2, 3, 4, 5])
                for a in range(2):
                    mm(banks[b1], s9v[:, a, 1024 + pr * 128:1024 + (pr + 1) * 128], cqnT[:, a, :], a == 0, a == 1,
                       ['cqnT%d' % a, k9], [PB(b1)])
                b2 = nb([2, 3, 4, 5])
                for a in range(2):
                    mm(banks[b2], s9v[:, a, 1536 + pr * 128:1536 + (pr + 1) * 128], cqnT[:, a, :], a == 0, a == 1,
                       ['cqnT%d' % a, k9], [PB(b2)])
                tt('dve', rt[0], banks[b1], Ct, ALU.mult, [PB(b1), kC], ['rt0'])
                tt('dve', rt[1], banks[b2], St, ALU.mult, [PB(b2), kS], ['rt1'])
                tt('pool', qrT[:, pr, :], rt[0], rt[1], ALU.add, ['rt0', 'rt1'], ['qrT%d' % pr])
            if g == 0 and seq == 0:
                dump('qpT', qpT, ['qpT%d' % q for q in range(8)], [128, 8, GT], BF16)
                dump('qrT', qrT, ['qrT%d' % q for q in range(4)], [128, 4, GT], BF16)

            J = 4 * g + 4
            steps = [(h, j) for h in range(8) for j in range(J)]
            sbank = {}
            ptbuf = {}

            def emit_S(n):
                h, j = steps[n]
                pr, hp = h // 2, h % 2
                r = j - 4 * g
                c0 = max(r, 0) * 128
                b = [0, 1, 2][n % 3]
                sbank[n] = b
                mm(banks[b][:, c0:GT], ckvnT[:, j * 128:(j + 1) * 128], qpT[:, h, c0:GT], True, False,
                   ['ckvnT', 'qpT%d' % h], [PB(b)])
                kk = krT if hp == 0 else krB
                mm(banks[b][:, c0:GT], kk[:, j * 128:(j + 1) * 128], qrT[:, pr, c0:GT],
                   False, r < 0, ['krT', 'krB', 'qrT%d' % pr], [PB(b)])
                if r >= 0:
                    mm(banks[b][:, c0:c0 + 128], ident, tri, False, True, ['ident', 'tri'], [PB(b)])
                p = n % 4
                ptbuf[n] = p
                act(PT[p][:, c0:GT], banks[b][:, c0:GT], AF.Exp, [PB(b)], ['PT%d' % p], scale=SCALE)

            def emit_PV(n):
                h, j = steps[n]
                r = j - 4 * g
                c0 = max(r, 0) * 128
                p = ptbuf[n]
                bo, br = 3 + (h % 2), 5 + (h % 2)
                mm(banks[bo][:, c0:GT], Vc[:, j, :], PT[p][:, c0:GT], j == 0, j == J - 1, ['Vc', 'PT%d' % p], [PB(bo)])
                mm(banks[br][:, c0:GT], ones, PT[p][:, c0:GT], j == 0, j == J - 1, ['ones', 'PT%d' % p], [PB(br)])
                if j == J - 1:
                    act(rcp[h % 2], banks[br], AF.Ln, [PB(br)], ['rcp%d_l' % (h % 2)])
                    act(rcp[h % 2], rcp[h % 2], AF.Exp, ['rcp%d_l' % (h % 2)], ['rcp%d' % (h % 2)], scale=-1.0)
                    tt('dve', oln[h % 2], banks[bo], rcp[h % 2], ALU.mult, [PB(bo), 'rcp%d' % (h % 2)], ['oln%d' % (h % 2)])

            def emit_O(h):
                mm(banks[7], wsm[:, 1, h, :], oln[h % 2], True, True, ['wsm', 'oln%d' % (h % 2)], [PB(7)])
                tt('dve', mb32[h % 2], banks[7], gbT[:, h, :], ALU.mult, [PB(7), 'gbT%d' % h], ['mb%d' % (h % 2)])
                tt('pool', uT[:, h, :], uT[:, h, :], mb32[h % 2], ALU.add, ['uT%d' % h, 'mb%d' % (h % 2)], ['uT%d' % h])

            NS = len(steps)
            LA = 2
            for n in range(min(LA, NS)):
                emit_S(n)
            pending_O = None
            for n in range(NS):
                if n + LA < NS:
                    emit_S(n + LA)
                emit_PV(n)
                h, j = steps[n]
                if pending_O is not None and (j == min(3, J - 1)):
                    emit_O(pending_O)
                    pending_O = None
                if j == J - 1:
                    pending_O = h
            emit_O(pending_O)
            if g == 0 and seq == 0:
                dump('merged', uT, ['uT%d' % q for q in range(8)], [128, 8, GT], BF16)

            if pending_store[0] is not None:
                pending_store[0]()
                pending_store[0] = None
            for i in range(4):
                dma('sp', x1[:, i, :], x_d[seq, t0 + i * 128:t0 + (i + 1) * 128, :], 'x1_%d' % i, [], ['x1_%d' % i])
            wo = [next_slot(10), next_slot(11)]
            for i in range(4):
                for half in range(2):
                    sl, sk = wo[half]
                    slv = sl.rearrange("p (k c) -> p k c", k=8)
                    b = nb([2, 3, 4, 5])
                    for k in range(8):
                        mm(banks[b], uT[:, k, i * 128:(i + 1) * 128], slv[:, k, :], k == 0, k == 7, ['uT%d' % k, sk], [PB(b)])
                    tt('dve', x1[:, i, half * 512:(half + 1) * 512], banks[b], x1[:, i, half * 512:(half + 1) * 512], ALU.add,
                       [PB(b), 'x1_%d' % i], ['x1_%d' % i])
                rms_pre(i, x1[:, i, :], 'x1_%d' % i, 8, 'n2')
                if i >= 1:
                    rms_pe(i - 1, C_G2)
            rms_pe(3, C_G2)
            if g == 0 and seq == 0:
                dump('x1', x1, ['x1_%d' % i for i in range(4)], [128, 4, D], F32)

            nxt = gi + 1
            has_next = nxt < n_seq * n_grp
            pend_mul = [None]
            for cpi in range(11):
                if has_next and cpi == 6:
                    m1_load(nxt // n_grp, nxt % n_grp, 0)
                sl, sk = next_slot(12 + cpi)
                slv = sl.rearrange("p (k c) -> p k c", k=8)
                for cc in range(2):
                    c = 2 * cpi + cc
                    ybs = []
                    bks = []
                    for X in range(2):
                        q = cc * 2 + X
                        b = nb([0, 1, 2, 3, 4, 5, 6, 7])
                        bks.append(b)
                        for k in range(8):
                            mm(banks[b], slv[:, k, q * 128:(q + 1) * 128], hT[:, k, :], k == 0, k == 7, ['hT', sk], [PB(b)])
                    if pend_mul[0] is not None:
                        pend_mul[0]()
                        pend_mul[0] = None
                    cpar = gi % 2
                    for X in range(2):
                        cidx = c + 22 * X
                        b = bks[X]
                        yv = yb[X][c % 2]
                        ykey = 'y%d_%d' % (X, c % 2)
                        act(yv, banks[b], AF.Identity, [PB(b), 'cols'], [ykey],
                            scale=col(C_CW + cidx * 3 + 2), bias=col(C_CB + cidx))
                        act(carry2[cpar][:, cidx, :], banks[b][:, GT - 2:GT], AF.Copy, [PB(b)], ['carry%d_%d' % (cpar, cidx)])
                        stt(yv[:, 1:GT], banks[b][:, 0:GT - 1], col(C_CW + cidx * 3 + 1), yv[:, 1:GT], ALU.mult, ALU.add,
                            [PB(b), ykey, 'cols'], [ykey])
                        stt(yv[:, 2:GT], banks[b][:, 0:GT - 2], col(C_CW + cidx * 3 + 0), yv[:, 2:GT], ALU.mult, ALU.add,
                            [PB(b), ykey, 'cols'], [ykey])
                        if g > 0:
                            cold = carry2[1 - cpar][:, cidx, :]
                            ckey = 'carry%d_%d' % (1 - cpar, cidx)
                            stt(yv[:, 0:2], cold, col(C_CW + cidx * 3 + 0), yv[:, 0:2], ALU.mult, ALU.add, [ckey, ykey, 'cols'], [ykey])
                            stt(yv[:, 0:1], cold[:, 1:2], col(C_CW + cidx * 3 + 1), yv[:, 0:1], ALU.mult, ALU.add, [ckey, ykey, 'cols'], [ykey])
                        ybs.append((yv, ykey))
                    act(sg[c % 2], ybs[0][0], AF.Silu, [ybs[0][1]], ['sg%d' % (c % 2)])

                    def mul_later(c=c, yv=ybs[1][0], ykey=ybs[1][1]):
                        tt('pool', aT[:, c, :], sg[c % 2], yv, ALU.mult, ['sg%d' % (c % 2), ykey], ['aT%d' % (c // 8)])
                    pend_mul[0] = mul_later
            pend_mul[0]()
            pend_mul[0] = None
            if g == 0 and seq == 0:
                dump('aT', aT[:, 0:22, :], ['aT0', 'aT1', 'aT2'], [128, 22, GT], BF16)
            if has_next:
                m1_pre(nxt // n_grp, nxt % n_grp, 0)
            for half in range(2):
                accb = [4, 5, 6, 7] if half == 0 else [0, 1, 2, 3]
                for m in range(3):
                    sl, sk = next_slot(23 + 3 * half + m)
                    slv = sl.rearrange("p (k c) -> p k c", k=8)
                    if half == 0 and m == 2 and has_next:
                        m1_load(nxt // n_grp, nxt % n_grp, 1)
                        m1_pre(nxt // n_grp, nxt % n_grp, 1)
                        rope_tables(nxt // n_grp, nxt % n_grp, nxt % 2)
                    for cc in range(8):
                        c = 8 * m + cc
                        if c >= 22:
                            break
                        for i in range(4):
                            mm(banks[accb[i]], aT[:, c, i * 128:(i + 1) * 128], slv[:, cc, :], c == 0, c == 21,
                               ['aT%d' % (c // 8), sk], [PB(accb[i])])
                for i in range(4):
                    tt('dve', x1[:, i, half * 512:(half + 1) * 512], banks[accb[i]], x1[:, i, half * 512:(half + 1) * 512], ALU.add,
                       [PB(accb[i]), 'x1_%d' % i], ['x1_%d' % i])
            if has_next:
                m1_pe()

            def final_norm(i, seq=seq, t0=t0):
                act(junk, x1[:, i, :], AF.Square, ['x1_%d' % i], ['n3ss%d' % i, 'junk'], accum=ss[:, 16 + i:17 + i])
                ts('dve', ss[:, 16 + i:17 + i], ss[:, 16 + i:17 + i], 1.0 / D, ALU.mult, ['n3ss%d' % i], ['n3ss%d_ms' % i], s2=EPS, op1=ALU.add)
                rsqrt_small(ss[:, 20 + i:21 + i], ss[:, 16 + i:17 + i], 1, 'n3ss%d' % i)
                stt(x1[:, i, :], x1[:, i, :], ss[:, 20 + i:21 + i], fng, ALU.mult, ALU.mult,
                    ['x1_%d' % i, 'n3ss%d_rs' % i, 'fng'], ['x1_%d' % i])

            def final_store(seq=seq, t0=t0):
                for i in range(4):
                    dma('sp', out_d[seq, t0 + i * 128:t0 + (i + 1) * 128, :], x1[:, i, :], 'st%d' % i, ['x1_%d' % i], [])
            pending_final[0] = final_norm
            pending_store[0] = final_store
    for i in range(4):
        pending_final[0](i)
    pending_store[0]()
    S.emit()
    return nc, S


def _kc(W):
    K, C = W.shape
    return np.ascontiguousarray(W.reshape(K // 128, 128, C).transpose(1, 0, 2)).reshape(128, -1)


def prep_weights(inp):
    f = np.float32
    w_in = np.asarray(inp["w_in"], f)[0]
    w_uq = np.asarray(inp["w_uq"], f)[0]
    w_ukv = np.asarray(inp["w_ukv"], f)[0]
    w_out = np.asarray(inp["w_out"], f)[0]
    w_up = np.asarray(inp["w_up"], f)[0]
    w_down = np.asarray(inp["w_down"], f)[0]
    wts = np.zeros((NSLOT, 128, 4096), f)
    U0, V0, CQ0, CKV0, KR0, GA0, GB0 = 0, 1024, 2048, 2304, 2432, 2496, 3520
    wts[0] = _kc(w_in[:, V0:V0 + 512])
    wts[1] = _kc(w_in[:, V0 + 512:V0 + 1024])
    kr = w_in[:, KR0:KR0 + 64]
    krp = np.concatenate([kr[:, 32:64], kr[:, 0:32]], axis=1)
    wts[2] = _kc(np.concatenate([w_in[:, CQ0:CQ0 + 256], w_in[:, CKV0:CKV0 + 128], kr, krp], axis=1))
    wts[3] = _kc(w_in[:, U0:U0 + 512])
    wts[4] = _kc(w_in[:, U0 + 512:U0 + 1024])
    wts[5] = _kc(w_in[:, GA0:GA0 + 512])
    wts[6] = _kc(w_in[:, GA0 + 512:GA0 + 1024])
    wts[7] = _kc(w_in[:, GB0:GB0 + 512])
    wts[8] = _kc(w_in[:, GB0 + 512:GB0 + 1024])
    nope = [w_uq[:, h * 192:h * 192 + 128] for h in range(8)]
    rope = [w_uq[:, h * 192 + 128:h * 192 + 192] for h in range(8)]
    ropep = [np.concatenate([r[:, 32:64], r[:, 0:32]], axis=1) for r in rope]
    wq = np.concatenate(nope + rope + ropep, axis=1)
    wts[9] = np.ascontiguousarray(wq.reshape(2, 128, 2048).transpose(1, 0, 2)).reshape(128, 4096)
    wts[10] = _kc(w_out[:, 0:512])
    wts[11] = _kc(w_out[:, 512:1024])
    for cpi in range(11):
        c0, c1 = 2 * cpi, 2 * cpi + 1
        blk = np.concatenate([w_up[:, c0 * 128:(c0 + 1) * 128], w_up[:, 2816 + c0 * 128:2816 + (c0 + 1) * 128],
                              w_up[:, c1 * 128:(c1 + 1) * 128], w_up[:, 2816 + c1 * 128:2816 + (c1 + 1) * 128]], axis=1)
        wts[12 + cpi] = _kc(blk)
    wd = w_down.reshape(22, 128, 1024)
    for half in range(2):
        for m in range(3):
            t = np.zeros((128, 8, 512), f)
            for cc in range(8):
                c = 8 * m + cc
                if c < 22:
                    t[:, cc, :] = wd[c, :, half * 512:(half + 1) * 512]
            wts[23 + 3 * half + m] = t.reshape(128, 4096)
    ukv = w_ukv.reshape(128, 8, 256)
    WukT = np.ascontiguousarray(ukv[:, :, 0:128].transpose(2, 1, 0))
    Wuv = np.ascontiguousarray(ukv[:, :, 128:256])
    spWT = np.ascontiguousarray(np.asarray(inp["a_spatial_w"], f)[0].transpose(2, 0, 1))
    wsm = np.concatenate([WukT.reshape(128, 1024), Wuv.reshape(128, 1024), spWT.reshape(128, 1024)], axis=1)
    cols = np.zeros((128, NCOL), f)
    cols[:, C_EPS] = EPS
    invf = (1.0 / (10000.0 ** (np.arange(0, 64, 2, dtype=np.float32) / 64.0))).astype(f)
    p = np.arange(128)
    cols[:, C_INVF] = invf[p % 32]
    cols[:, C_SGN] = np.where((p % 64) < 32, -1.0, 1.0)
    cols[:, C_G1:C_G1 + 8] = np.asarray(inp["mix_norm"], f)[0].reshape(8, 128).T
    cols[:, C_G2:C_G2 + 8] = np.asarray(inp["ffn_norm"], f)[0].reshape(8, 128).T
    cols[:, C_QG:C_QG + 2] = np.asarray(inp["q_a_norm"], f)[0].reshape(2, 128).T
    cols[:, C_KG] = np.asarray(inp["kv_a_norm"], f)[0]
    cols[:, C_CB:C_CB + 44] = np.asarray(inp["conv_b"], f)[0].reshape(44, 128).T
    cols[:, C_CW:C_CW + 132] = np.asarray(inp["conv_w"], f)[0].reshape(3, 44, 128).transpose(2, 1, 0).reshape(128, 132)
    rows = np.stack([np.asarray(inp["a_v_norm_g"], f)[0], np.asarray(inp["a_v_norm_b"], f)[0],
                     np.asarray(inp["final_norm"], f)], axis=0)
    bsr = np.asarray(inp["a_spatial_b"], f)[0].reshape(1, 1024)
    return dict(wts=wts, wsm=np.ascontiguousarray(wsm), cols=cols, rows=np.ascontiguousarray(rows), bsr=np.ascontiguousarray(bsr))


_CACHE = {}


def kernel(**inputs):
    x = np.asarray(inputs["x"], np.float32)
    pos = np.asarray(inputs["positions"], np.int32)
    w = prep_weights(inputs)
    if "nc" not in _CACHE:
        _CACHE["nc"] = build_program()[0]
    nc = _CACHE["nc"]
    n = 8
    in_maps = []
    for c in range(n):
        m = dict(w)
        m["x"] = np.ascontiguousarray(x[2 * c:2 * c + 2])
        m["pos"] = np.ascontiguousarray(pos[2 * c:2 * c + 2])
        in_maps.append(m)
    res = run_bass_kernel_spmd(nc, in_maps, core_ids=list(range(n)))
    return np.concatenate([np.asarray(r["out"], np.float32) for r in res.results], axis=0)
```

```python
import contextlib
import numpy as np
import concourse.bass as bass
import concourse.mybir as mybir
from concourse.bass_utils import run_bass_kernel_spmd

F32 = mybir.dt.float32
BF16 = mybir.dt.bfloat16
I32 = mybir.dt.int32
AF = mybir.ActivationFunctionType
ALU = mybir.AluOpType

S_LEN = 2048
D = 1024
GT = 512
NSLOT = 29
RING = 3
SCALE = 192.0 ** -0.5
EPS = 1e-6
MAGIC = 12582912.0
TWO_PI = float(2.0 * np.pi)
NCOL = 200
C_EPS, C_INVF, C_SGN, C_G1, C_G2, C_QG, C_KG, C_CB, C_CW = 0, 1, 2, 4, 12, 20, 22, 23, 67


class Sched:
    def __init__(self, nc):
        self.nc = nc
        self.ops = []
        self.last_w = {}
        self.readers = {}

    def op(self, eng, fn, reads=(), writes=(), dma_key=None, extra=()):
        idx = len(self.ops)
        deps = set(extra)
        for r in reads:
            if r in self.last_w:
                deps.add(self.last_w[r])
        for w in writes:
            if w in self.last_w:
                deps.add(self.last_w[w])
            for rd in self.readers.get(w, {}).values():
                deps.add(rd)
        deps.discard(idx)
        rk = eng if dma_key is None else 'd:' + str(dma_key)
        for r in reads:
            self.readers.setdefault(r, {})[rk] = idx
        for w in writes:
            self.last_w[w] = idx
            self.readers[w] = {}
        self.ops.append(dict(eng=eng, fn=fn, deps=deps, dma_key=dma_key))
        return idx

    def emit(self, final_wait_eng='sp'):
        nc = self.nc
        ops = self.ops
        engs = ['pe', 'act', 'dve', 'pool', 'sp']
        need = [False] * len(ops)
        for i, o in enumerate(ops):
            for d in o['deps']:
                p = ops[d]
                if p['dma_key'] is None and p['eng'] == 'pe' and o['eng'] == 'pe':
                    continue
                need[d] = True
        dma_keys = []
        for o in ops:
            if o['dma_key'] is not None and o['dma_key'] not in dma_keys:
                dma_keys.append(o['dma_key'])
        cnt = {e: 0 for e in engs}
        dcnt = {k: 0 for k in dma_keys}
        for i, o in enumerate(ops):
            if o['dma_key'] is not None:
                dcnt[o['dma_key']] += 16
                o['ms'] = ('d:' + str(o['dma_key']), dcnt[o['dma_key']])
            elif need[i]:
                cnt[o['eng']] += 1
                o['ms'] = ('e:' + o['eng'], cnt[o['eng']])
            else:
                o['ms'] = None
        sem_names = ['e:' + e for e in engs if cnt[e] > 0] + ['d:' + str(k) for k in dma_keys]
        self.n_sems = len(sem_names)
        self.counts = dict(cnt)
        with contextlib.ExitStack() as st:
            sems = {}
            for j, n in enumerate(sem_names):
                sems[n] = st.enter_context(nc.semaphore('s%d' % j))
            block = st.enter_context(nc.Block())
            per_eng = {e: [i for i, o in enumerate(ops) if o['eng'] == e] for e in engs}
            final = [('d:' + str(k), dcnt[k]) for k in dma_keys]

            def make(ename):
                def body(eng):
                    waited = {}
                    for i in per_eng[ename]:
                        o = ops[i]
                        for d in sorted(o['deps']):
                            p = ops[d]
                            if p['dma_key'] is None and p['eng'] == 'pe' and ename == 'pe':
                                continue
                            sn, val = p['ms']
                            if waited.get(sn, 0) >= val:
                                continue
                            eng.wait_ge(sems[sn], val)
                            waited[sn] = val
                        ins = o['fn'](eng)
                        if o['ms'] is not None:
                            sn, val = o['ms']
                            ins.then_inc(sems[sn], 16 if sn.startswith('d:') else 1)
                    if ename == final_wait_eng:
                        for sn, val in final:
                            if waited.get(sn, 0) < val:
                                eng.wait_ge(sems[sn], val)
                return body

            block.sync(make('sp'))
            if per_eng['pe']:
                block.tensor(make('pe'))
            if per_eng['act']:
                block.scalar(make('act'))
            if per_eng['dve']:
                block.vector(make('dve'))
            if per_eng['pool']:
                block.gpsimd(make('pool'))


def build_program(n_seq=2, n_grp=4, dbg=None):
    nc = bass.Bass("TRN2", target_bir_lowering=False)
    x_d = nc.dram_tensor("x", [n_seq, S_LEN, D], F32, kind="ExternalInput").ap()
    pos_d = nc.dram_tensor("pos", [n_seq, S_LEN], I32, kind="ExternalInput").ap()
    wts_d = nc.dram_tensor("wts", [NSLOT, 128, 4096], F32, kind="ExternalInput").ap()
    wsm_d = nc.dram_tensor("wsm", [128, 3072], F32, kind="ExternalInput").ap()
    cols_d = nc.dram_tensor("cols", [128, NCOL], F32, kind="ExternalInput").ap()
    rows_d = nc.dram_tensor("rows", [3, D], F32, kind="ExternalInput").ap()
    bsr_d = nc.dram_tensor("bsr", [1, D], F32, kind="ExternalInput").ap()
    out_d = nc.dram_tensor("out", [n_seq, S_LEN, D], F32, kind="ExternalOutput").ap()
    wtb_d = nc.dram_tensor("wtb", [NSLOT, 128, 4096], BF16, kind="Internal").ap()

    S = Sched(nc)
    _n = [0]

    def sb(shape, dt, name=None):
        _n[0] += 1
        return nc.alloc_sbuf_tensor("sb_" + (name or ("t%d" % _n[0])), shape, dt).ap()

    def mm(out, lhsT, rhs, start, stop, reads, writes):
        S.op('pe', lambda e: e.matmul(out, lhsT=lhsT, rhs=rhs, start=start, stop=stop,
                                      skip_group_check=True), reads=reads, writes=writes)

    def tr(out, in_, reads, writes):
        S.op('pe', lambda e: e.transpose(out, in_, ident), reads=list(reads) + ['ident'], writes=writes)

    def act(out, in_, func, reads, writes, scale=None, bias=None, accum=None):
        def f(e):
            kw = {}
            if scale is not None:
                kw['scale'] = scale
            if bias is not None:
                kw['bias'] = bias
            if accum is not None:
                kw['accum_out'] = accum
            return e.activation(out=out, in_=in_, func=func, **kw)
        S.op('act', f, reads=reads, writes=writes)

    def tt(eng, out, in0, in1, op, reads, writes):
        S.op(eng, lambda e: e.tensor_tensor(out=out, in0=in0, in1=in1, op=op), reads=reads, writes=writes)

    def ts(eng, out, in0, s1, op0, reads, writes, s2=None, op1=None):
        def f(e):
            if op1 is None:
                return e.tensor_scalar(out=out, in0=in0, scalar1=s1, scalar2=None, op0=op0)
            return e.tensor_scalar(out=out, in0=in0, scalar1=s1, scalar2=s2, op0=op0, op1=op1)
        S.op(eng, f, reads=reads, writes=writes)

    def stt(out, in0, scalar, in1, op0, op1, reads, writes):
        S.op('dve', lambda e: e.scalar_tensor_tensor(out=out, in0=in0, scalar=scalar, in1=in1, op0=op0, op1=op1),
             reads=reads, writes=writes)

    def cp(eng, out, in_, reads, writes):
        S.op(eng, lambda e: e.tensor_copy(out=out, in_=in_), reads=reads, writes=writes)

    sp_hist = []
    pool_hist = []
    MAX_OUTSTANDING = 6

    def dma(eng, out, in_, key, reads, writes):
        extra = ()
        if eng == 'sp' and len(sp_hist) >= MAX_OUTSTANDING:
            extra = (sp_hist[-MAX_OUTSTANDING],)
        if eng == 'pool' and len(pool_hist) >= 2:
            extra = (pool_hist[-2],)
        idx = S.op(eng, lambda e: e.dma_start(out=out, in_=in_), reads=reads, writes=writes, dma_key=key, extra=extra)
        if eng == 'sp':
            sp_hist.append(idx)
        if eng == 'pool':
            pool_hist.append(idx)

    def dump(name, ap, reads, shape, dt=F32):
        if dbg is None or name not in dbg:
            return
        d = nc.dram_tensor("dbg_" + name, list(shape), dt, kind="ExternalOutput").ap()
        dma('sp', d, ap, 'dbg_' + name, reads, [])

    banks = [nc.alloc_psum_tensor("bank%d" % b, [128, 512], F32).ap() for b in range(8)]
    banks_bf = [bk.bitcast(BF16) for bk in banks]

    def PB(b):
        return 'ps%d' % b

    colsT = sb([128, NCOL], F32, "cols")
    lng = sb([128, D], F32, "lng")
    lnb = sb([128, D], F32, "lnb")
    fng = sb([128, D], F32, "fng")
    ident = sb([128, 128], BF16, "ident")
    ones = sb([128, 128], BF16, "ones")
    tri = sb([128, 128], BF16, "tri")
    ones64 = sb([64, 128], BF16, "ones64")
    bsT = sb([64, D], BF16, "bsT")
    negh = sb([128, 8], F32, "negh")
    wsm = sb([128, 3, 8, 128], BF16, "wsm")
    ring = [sb([128, 4096], BF16, "ring%d" % r) for r in range(RING)]
    ckvnT = sb([128, S_LEN], BF16, "ckvnT")
    Vc = sb([128, 16, 128], BF16, "Vc")
    krT = sb([128, S_LEN], BF16, "krT")
    krB = sb([128, S_LEN], BF16, "krB")
    x1 = sb([128, 4, D], F32, "x1")
    carry2 = [sb([128, 44, 2], F32, "carry%d" % i) for i in range(2)]
    xs = [sb([128, D], F32, "xs%d" % i) for i in range(2)]
    junk = sb([128, D], BF16, "junk")
    hT = sb([128, 8, GT], BF16, "hT")
    ss = sb([128, 32], F32, "ss")
    bufA = sb([128, 3, 8, GT], BF16, "bufA")
    uT = bufA[:, 0]
    gbT = bufA[:, 1]
    qpT = bufA[:, 2]
    aT = bufA.rearrange("p a k t -> p (a k) t")
    gaT = [sb([128, GT], BF16, "gaT%d" % i) for i in range(2)]
    v32 = [sb([128, D], F32, "v32_%d" % i) for i in range(2)]
    vn = [sb([128, D], BF16, "vn%d" % i) for i in range(4)]
    hbf = vn
    stats = sb([128, 2, 6], F32, "stats")
    mv = sb([128, 4], F32, "mv")
    bufB = sb([128, 5632], F32, "bufB")
    cq32 = bufB[:, 0:1024].rearrange("p (a t) -> p a t", a=2)
    rstdq = bufB[:, 1024:2048].rearrange("p (a t) -> p a t", a=2)
    ckv32 = bufB[:, 2048:2560]
    rt = [bufB[:, 2560:3072], bufB[:, 3072:3584]]
    rcp = [bufB[:, 3584:4096], bufB[:, 4096:4608]]
    mb32 = [bufB[:, 4608:5120], bufB[:, 5120:5632]]
    yb = [[bufB[:, 2056 + (a * 2 + b) * 512:2056 + (a * 2 + b + 1) * 512] for b in range(2)] for a in range(2)]
    sg = [bufB[:, 4104:4616], bufB[:, 4616:5128]]
    sq = sb([128, 3, GT], BF16, "sq")
    cqnT = sb([128, 2, GT], BF16, "cqnT")
    CS = [[sb([128, GT], F32, "C%d" % i), sb([128, GT], F32, "S%d" % i)] for i in range(2)]
    posi = sb([128, GT], I32, "posi")
    ang = sb([128, GT], F32, "ang")
    angt = sb([128, GT], F32, "angt")
    qn = [sb([128, GT], BF16, "qn%d" % i) for i in range(2)]
    qrT = sb([128, 4, GT], BF16, "qrT")
    PT = [sb([128, GT], BF16, "PT%d" % i) for i in range(4)]
    oln = [sb([128, GT], BF16, "oln%d" % i) for i in range(2)]
    bstage = v32[0][0:64, :]
    identf = v32[1][:, 0:128]
    trif = v32[1][:, 128:256]
    bstmp = vn[0][0:64, :]

    def col(c, n=1):
        return colsT[:, c:c + n]

    cast_done = [0]
    CAST_AHEAD = 6

    def cast_upto(n):
        while cast_done[0] < min(n, NSLOT):
            s_ = cast_done[0]
            dma('pool', wtb_d[s_], wts_d[s_], 'cast%d' % s_, [], ['wtb%d' % s_])
            cast_done[0] += 1
    dma('sp', colsT, cols_d, 'cols', [], ['cols'])
    dma('sp', lng, rows_d[0:1, :].partition_broadcast(128), 'lng', [], ['lng'])
    dma('sp', lnb, rows_d[1:2, :].partition_broadcast(128), 'lnb', [], ['lnb'])
    dma('sp', fng, rows_d[2:3, :].partition_broadcast(128), 'fng', [], ['fng'])
    dma('sp', bstage[0:1, :], bsr_d, 'bs0', [], ['v32_0h0'])
    dma('sp', bstage[32:33, :], bsr_d, 'bs32', [], ['v32_0h1'])
    dma('pool', wsm.rearrange("p a h l -> p (a h l)"), wsm_d, 'wsm', [], ['wsm'])
    S.op('pool', lambda e: e.memset(identf, 1.0), writes=['v32_1h0'])
    S.op('pool', lambda e: e.affine_select(out=identf, in_=identf, pattern=[[-1, 128]], compare_op=ALU.is_equal,
                                           fill=0.0, base=0, channel_multiplier=1), reads=['v32_1h0'], writes=['v32_1h0'])
    cp('pool', ident, identf, ['v32_1h0'], ['ident'])
    S.op('pool', lambda e: e.memset(ones, 1.0), writes=['ones'])
    S.op('pool', lambda e: e.memset(negh, -0.5), writes=['negh'])
    S.op('pool', lambda e: e.memset(trif, 0.0), writes=['v32_1h0'])
    S.op('pool', lambda e: e.affine_select(out=trif, in_=trif, pattern=[[1, 128]], compare_op=ALU.is_ge,
                                           fill=-30000.0, base=0, channel_multiplier=-1), reads=['v32_1h0'], writes=['v32_1h0'])
    cp('pool', tri, trif, ['v32_1h0'], ['tri'])
    spv = wsm[:, 2]
    S.op('pool', lambda e: e.affine_select(out=spv, in_=spv, pattern=[[0, 8], [1, 128]], compare_op=ALU.is_ge,
                                           fill=0.0, base=0, channel_multiplier=-1), reads=['wsm'], writes=['wsm'])
    S.op('pool', lambda e: e.memset(ones64, 0.0), writes=['ones64'])
    S.op('pool', lambda e: e.memset(ones64[0:1, :], 1.0), reads=['ones64'], writes=['ones64'])
    S.op('pool', lambda e: e.memset(ones64[32:33, :], 1.0), reads=['ones64'], writes=['ones64'])
    S.op('pool', lambda e: e.memset(bsT, 0.0), writes=['bsT'])
    cp('dve', bsT[0:1, :], bstage[0:1, :], ['v32_0h0', 'bsT'], ['bsT'])
    cp('dve', bstmp[32:33, :], bstage[32:33, :], ['v32_0h1'], ['vn0'])
    tt('dve', bsT[32:33, :], bstage[32:33, :], bstmp[32:33, :], ALU.subtract, ['v32_0h1', 'vn0', 'bsT'], ['bsT'])
    S.op('pool', lambda e: e.memset(krT[64:128, :], 0.0), writes=['krT'])
    S.op('pool', lambda e: e.memset(krB[0:64, :], 0.0), writes=['krB'])
    cast_upto(CAST_AHEAD)

    ring_pos = [0]

    def next_slot(slot_id):
        cast_upto(slot_id + 1 + CAST_AHEAD)
        r = ring_pos[0] % RING
        ring_pos[0] += 1
        dma('sp', ring[r], wtb_d[slot_id], 'ring%d' % r, ['wtb%d' % slot_id], ['ring%d' % r])
        return ring[r], 'ring%d' % r

    bank_rr = [0]

    def nb(allowed):
        b = allowed[bank_rr[0] % len(allowed)]
        bank_rr[0] += 1
        return b

    def rsqrt_small(dst, src, n, key):
        tt('pool', dst, src, negh[:, 0:n], ALU.pow, [key + '_ms', 'negh'], [key + '_rs'])

    def rope_tables(seq, g, buf):
        t0 = g * GT
        Ct, St = CS[buf]
        kC, kS = 'C%d' % buf, 'S%d' % buf
        dma('sp', posi, pos_d[seq:seq + 1, t0:t0 + GT].partition_broadcast(128), 'pos', [], ['posi'])
        cp('dve', ang, posi, ['posi'], ['ang'])
        ts('dve', ang, ang, col(C_INVF), ALU.mult, ['ang', 'cols'], ['ang'])
        C1 = 6.28125
        C2 = float(2.0 * np.pi - 6.28125)

        def reduce_into(dst, dkey):
            ts('dve', angt, ang, 1.0 / TWO_PI, ALU.mult, ['ang'], ['angt'], s2=MAGIC, op1=ALU.add)
            ts('dve', angt, angt, MAGIC, ALU.subtract, ['angt'], ['angt'])
            stt(dst, angt, -C1, ang, ALU.mult, ALU.add, ['angt', 'ang'], [dkey])
            stt(dst, angt, -C2, dst, ALU.mult, ALU.add, ['angt', dkey], [dkey])
            ts('dve', dst, dst, float(np.pi), ALU.min, [dkey], [dkey], s2=-float(np.pi), op1=ALU.max)
        reduce_into(St, kS)
        act(St, St, AF.Sin, [kS, 'cols'], [kS], scale=col(C_SGN))
        ts('dve', ang, ang, float(np.pi / 2), ALU.add, ['ang'], ['ang'])
        reduce_into(Ct, kC)
        act(Ct, Ct, AF.Sin, [kC], [kC])

    def rms_pre(i, src, skey, ssbase, tag):
        c_s = ss[:, ssbase + i:ssbase + i + 1]
        c_r = ss[:, ssbase + 4 + i:ssbase + 5 + i]
        act(junk, src, AF.Square, [skey], [tag + 'ss%d' % i, 'junk'], accum=c_s)
        ts('dve', c_s, c_s, 1.0 / D, ALU.mult, [tag + 'ss%d' % i], [tag + 'ss%d_ms' % i], s2=EPS, op1=ALU.add)
        rsqrt_small(c_r, c_s, 1, tag + 'ss%d' % i)
        act(hbf[i], src, AF.Copy, [skey, tag + 'ss%d_rs' % i], ['vn%d' % i], scale=c_r)

    def rms_pe(i, gcol):
        b = nb([0, 1])
        for k in range(8):
            tr(banks_bf[b][:, k * 128:(k + 1) * 128], hbf[i][:, k * 128:(k + 1) * 128], ['vn%d' % i], [PB(b)])
        tt('dve', hT[:, :, i * 128:(i + 1) * 128],
           banks_bf[b].rearrange("p (k t) -> p k t", k=8),
           col(gcol, 8).unsqueeze(2).to_broadcast([128, 8, 128]), ALU.mult,
           [PB(b), 'cols'], ['hT'])

    def m1_load(seq, g, part):
        t0 = g * GT
        for i in (0, 1) if part == 0 else (2, 3):
            dma('sp', xs[i % 2], x_d[seq, t0 + i * 128:t0 + (i + 1) * 128, :], 'xs%d' % (i % 2), [], ['xs%d' % (i % 2)])

    def m1_pre(seq, g, part):
        for i in (0, 1) if part == 0 else (2, 3):
            rms_pre(i, xs[i % 2], 'xs%d' % (i % 2), 0, 'n1')

    def m1_pe():
        for i in range(4):
            rms_pe(i, C_G1)

    first = True
    pending_final = [None]
    pending_store = [None]
    for seq in range(n_seq):
        for g in range(n_grp):
            t0 = g * GT
            gi = seq * n_grp + g
            buf = gi % 2
            Ct, St = CS[buf]
            kC, kS = 'C%d' % buf, 'S%d' % buf
            if first:
                rope_tables(seq, g, buf)
                first = False
            if gi == 0:
                m1_load(seq, g, 0)
                m1_pre(seq, g, 0)
                m1_load(seq, g, 1)
                m1_pre(seq, g, 1)
                m1_pe()
            if g == 0 and seq == 0:
                dump('hT', hT, ['hT'], [128, 8, GT], BF16)

            s0, k0 = next_slot(0)
            s1, k1 = next_slot(1)
            s0v = s0.rearrange("p (k c) -> p k c", k=8)
            s1v = s1.rearrange("p (k c) -> p k c", k=8)
            for i in range(4):
                vb = v32[i % 2]
                vkey = 'v32_%d' % (i % 2)
                for half, (sv, sk) in enumerate(((s0v, k0), (s1v, k1))):
                    b = nb([2, 3, 4, 5])
                    for k in range(8):
                        mm(banks[b], hT[:, k, i * 128:(i + 1) * 128], sv[:, k, :], k == 0, k == 7, ['hT', sk], [PB(b)])
                    act(vb[:, half * 512:(half + 1) * 512], banks[b], AF.Gelu_apprx_tanh, [PB(b)], [vkey + 'h%d' % half])
                for half in range(2):
                    S.op('dve', (lambda vb=vb, half=half: (lambda e: e.bn_stats(out=stats[:, half, :], in_=vb[:, half * 512:(half + 1) * 512])))(),
                         reads=[vkey + 'h%d' % half], writes=['stats%d' % half])
                S.op('dve', lambda e: e.bn_aggr(out=mv[:, 0:2], in_=stats.rearrange("p a b -> p (a b)")),
                     reads=['stats0', 'stats1'], writes=['mv'])
                ts('dve', mv[:, 2:3], mv[:, 1:2], EPS, ALU.add, ['mv'], ['mv_ms'])
                rsqrt_small(mv[:, 3:4], mv[:, 2:3], 1, 'mv')
                ts('dve', vb, vb, mv[:, 0:1], ALU.subtract, [vkey + 'h0', vkey + 'h1', 'mv', 'mv_rs'], [vkey + 'h0', vkey + 'h1'],
                   s2=mv[:, 3:4], op1=ALU.mult)
                tt('pool', vb, vb, lng, ALU.mult, [vkey + 'h0', vkey + 'h1', 'lng'], [vkey + 'h0', vkey + 'h1'])
                tt('pool', vn[i], vb, lnb, ALU.add, [vkey + 'h0', vkey + 'h1', 'lnb'], ['vn%d' % i])
                if g == 0 and seq == 0 and i == 0:
                    dump('vn', vn[0], ['vn0'], [128, D], BF16)

            s2, k2 = next_slot(2)
            s2v = s2.rearrange("p (k c) -> p k c", k=8)
            for a in range(2):
                b = nb([2, 3, 4, 5])
                for k in range(8):
                    mm(banks[b], s2v[:, k, a * 128:(a + 1) * 128], hT[:, k, :], k == 0, k == 7, ['hT', k2], [PB(b)])
                act(cq32[:, a, :], banks[b], AF.Copy, [PB(b)], ['cq32_%d' % a])
                act(sq[:, a, :], banks[b], AF.Square, [PB(b)], ['sq%d' % a])
            b = nb([2, 3, 4, 5])
            for k in range(8):
                mm(banks[b], s2v[:, k, 256:384], hT[:, k, :], k == 0, k == 7, ['hT', k2], [PB(b)])
            act(ckv32, banks[b], AF.Copy, [PB(b)], ['ckv32'])
            act(sq[:, 2, :], banks[b], AF.Square, [PB(b)], ['sq2'])
            bkr = nb([2, 3, 4, 5])
            for k in range(8):
                mm(banks[bkr][0:64, :], s2v[:, k, 384:448], hT[:, k, :], k == 0, k == 7, ['hT', k2], [PB(bkr)])
            bkp = nb([2, 3, 4, 5])
            for k in range(8):
                mm(banks[bkp][0:64, :], s2v[:, k, 448:512], hT[:, k, :], k == 0, k == 7, ['hT', k2], [PB(bkp)])
            tt('dve', rt[0][0:64, :], banks[bkr][0:64, :], Ct[0:64, :], ALU.mult, [PB(bkr), kC], ['rt0'])
            tt('dve', rt[1][0:64, :], banks[bkp][0:64, :], St[0:64, :], ALU.mult, [PB(bkp), kS], ['rt1'])
            tt('pool', krT[0:64, t0:t0 + GT], rt[0][0:64, :], rt[1][0:64, :], ALU.add, ['rt0', 'rt1'], ['krT'])
            cp('dve', krB[64:128, t0:t0 + GT], krT[0:64, t0:t0 + GT], ['krT'], ['krB'])
            def emit_rq(t0=t0):
                b = nb([2, 3, 4, 5])
                for a in range(2):
                    mm(banks[b], ones, sq[:, a, :], a == 0, a == 1, ['ones', 'sq%d' % a], [PB(b)])
                act(rstdq[:, 0, :], banks[b], AF.Ln, [PB(b), 'cols'], ['rq0_ms'], scale=1.0 / 256, bias=col(C_EPS))
                act(rstdq[:, 0, :], rstdq[:, 0, :], AF.Exp, ['rq0_ms'], ['rq0_rs'], scale=-0.5)
                for a in range(2):
                    stt(cqnT[:, a, :], cq32[:, a, :], col(C_QG + a), rstdq[:, 0, :], ALU.mult, ALU.mult,
                        ['cq32_%d' % a, 'rq0_rs', 'cols'], ['cqnT%d' % a])
                b = nb([2, 3, 4, 5])
                mm(banks[b], ones, sq[:, 2, :], True, True, ['ones', 'sq2'], [PB(b)])
                act(rstdq[:, 1, :], banks[b], AF.Ln, [PB(b), 'cols'], ['rq1_ms'], scale=1.0 / 128, bias=col(C_EPS))
                act(rstdq[:, 1, :], rstdq[:, 1, :], AF.Exp, ['rq1_ms'], ['rq1_rs'], scale=-0.5)
                stt(ckvnT[:, t0:t0 + GT], ckv32, col(C_KG), rstdq[:, 1, :], ALU.mult, ALU.mult,
                    ['ckv32', 'rq1_rs', 'cols'], ['ckvnT'])
            def emit_vt(t0=t0, g=g):
                b = nb([0, 1])
                for i in range(4):
                    tr(banks_bf[b][:, i * 128:(i + 1) * 128], ckvnT[:, t0 + i * 128:t0 + (i + 1) * 128], ['ckvnT'], [PB(b)])
                cp('dve', Vc[:, 4 * g:4 * g + 4, :], banks_bf[b][:, 0:512].rearrange("p (i l) -> p i l", i=4), [PB(b)], ['Vc'])
            for sidx in range(3, 9):
                if sidx == 4:
                    emit_rq()
                if pending_final[0] is not None and sidx < 7:
                    pending_final[0](sidx - 3)
                    if sidx == 6:
                        pending_final[0] = None
                sl, sk = next_slot(sidx)
                slv = sl.rearrange("p (k c) -> p k c", k=8)
                for cc in range(4):
                    c = ((sidx - 3) % 2) * 4 + cc
                    b = nb([2, 3, 4, 5])
                    for k in range(8):
                        mm(banks[b], slv[:, k, cc * 128:(cc + 1) * 128], hT[:, k, :], k == 0, k == 7, ['hT', sk], [PB(b)])
                    if sidx < 5:
                        act(uT[:, c, :], banks[b], AF.Gelu_apprx_tanh, [PB(b)], ['uT%d' % c])
                    elif sidx < 7:
                        act(gaT[c % 2], banks[b], AF.Sigmoid, [PB(b)], ['gaT%d' % (c % 2)])
                        tt('pool', uT[:, c, :], uT[:, c, :], gaT[c % 2], ALU.mult, ['uT%d' % c, 'gaT%d' % (c % 2)], ['uT%d' % c])
                    else:
                        act(gbT[:, c, :], banks[b], AF.Sigmoid, [PB(b)], ['gbT%d' % c])

            emit_vt()
            if g == 0 and seq == 0:
                dump('ckvnT', ckvnT[:, 0:GT], ['ckvnT'], [128, GT], BF16)
                dump('krT', krT[:, 0:GT], ['krT'], [128, GT], BF16)

            for i in range(4):
                for hb2 in range(2):
                    b = nb([6, 7])
                    for gg in range(4):
                        gr = hb2 * 4 + gg
                        mm(banks[b][:, gg * 128:(gg + 1) * 128], vn[i][:, gr * 128:(gr + 1) * 128], wsm[:, 2, gr, :],
                           True, False, ['vn%d' % i, 'wsm'], [PB(b)])
                        mm(banks[b][:, gg * 128:(gg + 1) * 128], ones64, bsT[:, gr * 128:(gr + 1) * 128],
                           False, True, ['ones64', 'bsT'], [PB(b)])
                    tt('dve', uT[:, hb2 * 4:hb2 * 4 + 4, i * 128:(i + 1) * 128],
                       banks[b].rearrange("p (g t) -> p g t", g=4),
                       uT[:, hb2 * 4:hb2 * 4 + 4, i * 128:(i + 1) * 128], ALU.mult,
                       [PB(b)] + ['uT%d' % (hb2 * 4 + q) for q in range(4)], ['uT%d' % (hb2 * 4 + q) for q in range(4)])
            if g == 0 and seq == 0:
                dump('maT', uT, ['uT%d' % q for q in range(8)], [128, 8, GT], BF16)

            s9, k9 = next_slot(9)
            s9v = s9.rearrange("p (a c) -> p a c", a=2)
            def q_a(h):
                b = nb([2, 3, 4, 5])
                for a in range(2):
                    mm(banks[b], s9v[:, a, h * 128:(h + 1) * 128], cqnT[:, a, :], a == 0, a == 1, ['cqnT%d' % a, k9], [PB(b)])
                act(qn[h % 2], banks[b], AF.Copy, [PB(b)], ['qn%d' % (h % 2)])

            def q_b(h):
                b = nb([2, 3, 4, 5])
                mm(banks[b], wsm[:, 0, h, :], qn[h % 2], True, True, ['wsm', 'qn%d' % (h % 2)], [PB(b)])
                cp('dve', qpT[:, h, :], banks[b], [PB(b)], ['qpT%d' % h])
            q_a(0)
            for h in range(8):
                if h + 1 < 8:
                    q_a(h + 1)
                q_b(h)
            for pr in range(4):
                b1 = nb([2, 3, 4, 5])
                for a in range(2):
                    mm(banks[b1], s9v[:, a, 1024 + pr * 128:1024 + (pr + 1) * 128], cqnT[:, a, :], a == 0, a == 1,
                       ['cqnT%d' % a, k9], [PB(b1)])
                b2 = nb([2, 3, 4, 5])
                for a in range(2):
                    mm(banks[b2], s9v[:, a, 1536 + pr * 128:1536 + (pr + 1) * 128], cqnT[:, a, :], a == 0, a == 1,
                       ['cqnT%d' % a, k9], [PB(b2)])
                tt('dve', rt[0], banks[b1], Ct, ALU.mult, [PB(b1), kC], ['rt0'])
                tt('dve', rt[1], banks[b2], St, ALU.mult, [PB(b2), kS], ['rt1'])
                tt('pool', qrT[:, pr, :], rt[0], rt[1], ALU.add, ['rt0', 'rt1'], ['qrT%d' % pr])
            if g == 0 and seq == 0:
                dump('qpT', qpT, ['qpT%d' % q for q in range(8)], [128, 8, GT], BF16)
                dump('qrT', qrT, ['qrT%d' % q for q in range(4)], [128, 4, GT], BF16)

            J = 4 * g + 4
            steps = [(h, j) for h in range(8) for j in range(J)]
            sbank = {}
            ptbuf = {}

            def emit_S(n):
                h, j = steps[n]
                pr, hp = h // 2, h % 2
                r = j - 4 * g
                c0 = max(r, 0) * 128
                b = [0, 1, 2][n % 3]
                sbank[n] = b
                mm(banks[b][:, c0:GT], ckvnT[:, j * 128:(j + 1) * 128], qpT[:, h, c0:GT], True, False,
                   ['ckvnT', 'qpT%d' % h], [PB(b)])
                kk = krT if hp == 0 else krB
                mm(banks[b][:, c0:GT], kk[:, j * 128:(j + 1) * 128], qrT[:, pr, c0:GT],
                   False, r < 0, ['krT', 'krB', 'qrT%d' % pr], [PB(b)])
                if r >= 0:
                    mm(banks[b][:, c0:c0 + 128], ident, tri, False, True, ['ident', 'tri'], [PB(b)])
                p = n % 4
                ptbuf[n] = p
                act(PT[p][:, c0:GT], banks[b][:, c0:GT], AF.Exp, [PB(b)], ['PT%d' % p], scale=SCALE)

            def emit_PV(n):
                h, j = steps[n]
                r = j - 4 * g
                c0 = max(r, 0) * 128
                p = ptbuf[n]
                bo, br = 3 + (h % 2), 5 + (h % 2)
                mm(banks[bo][:, c0:GT], Vc[:, j, :], PT[p][:, c0:GT], j == 0, j == J - 1, ['Vc', 'PT%d' % p], [PB(bo)])
                mm(banks[br][:, c0:GT], ones, PT[p][:, c0:GT], j == 0, j == J - 1, ['ones', 'PT%d' % p], [PB(br)])
                if j == J - 1:
                    act(rcp[h % 2], banks[br], AF.Ln, [PB(br)], ['rcp%d_l' % (h % 2)])
                    act(rcp[h % 2], rcp[h % 2], AF.Exp, ['rcp%d_l' % (h % 2)], ['rcp%d' % (h % 2)], scale=-1.0)
                    tt('dve', oln[h % 2], banks[bo], rcp[h % 2], ALU.mult, [PB(bo), 'rcp%d' % (h % 2)], ['oln%d' % (h % 2)])

            def emit_O(h):
                mm(banks[7], wsm[:, 1, h, :], oln[h % 2], True, True, ['wsm', 'oln%d' % (h % 2)], [PB(7)])
                tt('dve', mb32[h % 2], banks[7], gbT[:, h, :], ALU.mult, [PB(7), 'gbT%d' % h], ['mb%d' % (h % 2)])
                tt('pool', uT[:, h, :], uT[:, h, :], mb32[h % 2], ALU.add, ['uT%d' % h, 'mb%d' % (h % 2)], ['uT%d' % h])

            NS = len(steps)
            LA = 2
            for n in range(min(LA, NS)):
                emit_S(n)
            pending_O = None
            for n in range(NS):
                if n + LA < NS:
                    emit_S(n + LA)
                emit_PV(n)
                h, j = steps[n]
                if pending_O is not None and (j == min(3, J - 1)):
                    emit_O(pending_O)
                    pending_O = None
                if j == J - 1:
                    pending_O = h
            emit_O(pending_O)
            if g == 0 and seq == 0:
                dump('merged', uT, ['uT%d' % q for q in range(8)], [128, 8, GT], BF16)

            if pending_store[0] is not None:
                pending_store[0]()
                pending_store[0] = None
            for i in range(4):
                dma('sp', x1[:, i, :], x_d[seq, t0 + i * 128:t0 + (i + 1) * 128, :], 'x1_%d' % i, [], ['x1_%d' % i])
            wo = [next_slot(10), next_slot(11)]
            for i in range(4):
                for half in range(2):
                    sl, sk = wo[half]
                    slv = sl.rearrange("p (k c) -> p k c", k=8)
                    b = nb([2, 3, 4, 5])
                    for k in range(8):
                        mm(banks[b], uT[:, k, i * 128:(i + 1) * 128], slv[:, k, :], k == 0, k == 7, ['uT%d' % k, sk], [PB(b)])
                    tt('dve', x1[:, i, half * 512:(half + 1) * 512], banks[b], x1[:, i, half * 512:(half + 1) * 512], ALU.add,
                       [PB(b), 'x1_%d' % i], ['x1_%d' % i])
                rms_pre(i, x1[:, i, :], 'x1_%d' % i, 8, 'n2')
                if i >= 1:
                    rms_pe(i - 1, C_G2)
            rms_pe(3, C_G2)
            if g == 0 and seq == 0:
                dump('x1', x1, ['x1_%d' % i for i in range(4)], [128, 4, D], F32)

            nxt = gi + 1
            has_next = nxt < n_seq * n_grp
            pend_mul = [None]
            for cpi in range(11):
                if has_next and cpi == 6:
                    m1_load(nxt // n_grp, nxt % n_grp, 0)
                sl, sk = next_slot(12 + cpi)
                slv = sl.rearrange("p (k c) -> p k c", k=8)
                for cc in range(2):
                    c = 2 * cpi + cc
                    ybs = []
                    bks = []
                    for X in range(2):
                        q = cc * 2 + X
                        b = nb([0, 1, 2, 3, 4, 5, 6, 7])
                        bks.append(b)
                        for k in range(8):
                            mm(banks[b], slv[:, k, q * 128:(q + 1) * 128], hT[:, k, :], k == 0, k == 7, ['hT', sk], [PB(b)])
                    if pend_mul[0] is not None:
                        pend_mul[0]()
                        pend_mul[0] = None
                    cpar = gi % 2
                    for X in range(2):
                        cidx = c + 22 * X
                        b = bks[X]
                        yv = yb[X][c % 2]
                        ykey = 'y%d_%d' % (X, c % 2)
                        act(yv, banks[b], AF.Identity, [PB(b), 'cols'], [ykey],
                            scale=col(C_CW + cidx * 3 + 2), bias=col(C_CB + cidx))
                        act(carry2[cpar][:, cidx, :], banks[b][:, GT - 2:GT], AF.Copy, [PB(b)], ['carry%d_%d' % (cpar, cidx)])
                        stt(yv[:, 1:GT], banks[b][:, 0:GT - 1], col(C_CW + cidx * 3 + 1), yv[:, 1:GT], ALU.mult, ALU.add,
                            [PB(b), ykey, 'cols'], [ykey])
                        stt(yv[:, 2:GT], banks[b][:, 0:GT - 2], col(C_CW + cidx * 3 + 0), yv[:, 2:GT], ALU.mult, ALU.add,
                            [PB(b), ykey, 'cols'], [ykey])
                        if g > 0:
                            cold = carry2[1 - cpar][:, cidx, :]
                            ckey = 'carry%d_%d' % (1 - cpar, cidx)
                            stt(yv[:, 0:2], cold, col(C_CW + cidx * 3 + 0), yv[:, 0:2], ALU.mult, ALU.add, [ckey, ykey, 'cols'], [ykey])
                            stt(yv[:, 0:1], cold[:, 1:2], col(C_CW + cidx * 3 + 1), yv[:, 0:1], ALU.mult, ALU.add, [ckey, ykey, 'cols'], [ykey])
                        ybs.append((yv, ykey))
                    act(sg[c % 2], ybs[0][0], AF.Silu, [ybs[0][1]], ['sg%d' % (c % 2)])

                    def mul_later(c=c, yv=ybs[1][0], ykey=ybs[1][1]):
                        tt('pool', aT[:, c, :], sg[c % 2], yv, ALU.mult, ['sg%d' % (c % 2), ykey], ['aT%d' % (c // 8)])
                    pend_mul[0] = mul_later
            pend_mul[0]()
            pend_mul[0] = None
            if g == 0 and seq == 0:
                dump('aT', aT[:, 0:22, :], ['aT0', 'aT1', 'aT2'], [128, 22, GT], BF16)
            if has_next:
                m1_pre(nxt // n_grp, nxt % n_grp, 0)
            for half in range(2):
                accb = [4, 5, 6, 7] if half == 0 else [0, 1, 2, 3]
                for m in range(3):
                    sl, sk = next_slot(23 + 3 * half + m)
                    slv = sl.rearrange("p (k c) -> p k c", k=8)
                    if half == 0 and m == 2 and has_next:
                        m1_load(nxt // n_grp, nxt % n_grp, 1)
                        m1_pre(nxt // n_grp, nxt % n_grp, 1)
                        rope_tables(nxt // n_grp, nxt % n_grp, nxt % 2)
                    for cc in range(8):
                        c = 8 * m + cc
                        if c >= 22:
                            break
                        for i in range(4):
                            mm(banks[accb[i]], aT[:, c, i * 128:(i + 1) * 128], slv[:, cc, :], c == 0, c == 21,
                               ['aT%d' % (c // 8), sk], [PB(accb[i])])
                for i in range(4):
                    tt('dve', x1[:, i, half * 512:(half + 1) * 512], banks[accb[i]], x1[:, i, half * 512:(half + 1) * 512], ALU.add,
                       [PB(accb[i]), 'x1_%d' % i], ['x1_%d' % i])
            if has_next:
                m1_pe()

            def final_norm(i, seq=seq, t0=t0):
                act(junk, x1[:, i, :], AF.Square, ['x1_%d' % i], ['n3ss%d' % i, 'junk'], accum=ss[:, 16 + i:17 + i])
                ts('dve', ss[:, 16 + i:17 + i], ss[:, 16 + i:17 + i], 1.0 / D, ALU.mult, ['n3ss%d' % i], ['n3ss%d_ms' % i], s2=EPS, op1=ALU.add)
                rsqrt_small(ss[:, 20 + i:21 + i], ss[:, 16 + i:17 + i], 1, 'n3ss%d' % i)
                stt(x1[:, i, :], x1[:, i, :], ss[:, 20 + i:21 + i], fng, ALU.mult, ALU.mult,
                    ['x1_%d' % i, 'n3ss%d_rs' % i, 'fng'], ['x1_%d' % i])

            def final_store(seq=seq, t0=t0):
                for i in range(4):
                    dma('sp', out_d[seq, t0 + i * 128:t0 + (i + 1) * 128, :], x1[:, i, :], 'st%d' % i, ['x1_%d' % i], [])
            pending_final[0] = final_norm
            pending_store[0] = final_store
    for i in range(4):
        pending_final[0](i)
    pending_store[0]()
    S.emit()
    return nc, S


def _kc(W):
    K, C = W.shape
    return np.ascontiguousarray(W.reshape(K // 128, 128, C).transpose(1, 0, 2)).reshape(128, -1)


def prep_weights(inp):
    f = np.float32
    w_in = np.asarray(inp["w_in"], f)[0]
    w_uq = np.asarray(inp["w_uq"], f)[0]
    w_ukv = np.asarray(inp["w_ukv"], f)[0]
    w_out = np.asarray(inp["w_out"], f)[0]
    w_up = np.asarray(inp["w_up"], f)[0]
    w_down = np.asarray(inp["w_down"], f)[0]
    wts = np.zeros((NSLOT, 128, 4096), f)
    U0, V0, CQ0, CKV0, KR0, GA0, GB0 = 0, 1024, 2048, 2304, 2432, 2496, 3520
    wts[0] = _kc(w_in[:, V0:V0 + 512])
    wts[1] = _kc(w_in[:, V0 + 512:V0 + 1024])
    kr = w_in[:, KR0:KR0 + 64]
    krp = np.concatenate([kr[:, 32:64], kr[:, 0:32]], axis=1)
    wts[2] = _kc(np.concatenate([w_in[:, CQ0:CQ0 + 256], w_in[:, CKV0:CKV0 + 128], kr, krp], axis=1))
    wts[3] = _kc(w_in[:, U0:U0 + 512])
    wts[4] = _kc(w_in[:, U0 + 512:U0 + 1024])
    wts[5] = _kc(w_in[:, GA0:GA0 + 512])
    wts[6] = _kc(w_in[:, GA0 + 512:GA0 + 1024])
    wts[7] = _kc(w_in[:, GB0:GB0 + 512])
    wts[8] = _kc(w_in[:, GB0 + 512:GB0 + 1024])
    nope = [w_uq[:, h * 192:h * 192 + 128] for h in range(8)]
    rope = [w_uq[:, h * 192 + 128:h * 192 + 192] for h in range(8)]
    ropep = [np.concatenate([r[:, 32:64], r[:, 0:32]], axis=1) for r in rope]
    wq = np.concatenate(nope + rope + ropep, axis=1)
    wts[9] = np.ascontiguousarray(wq.reshape(2, 128, 2048).transpose(1, 0, 2)).reshape(128, 4096)
    wts[10] = _kc(w_out[:, 0:512])
    wts[11] = _kc(w_out[:, 512:1024])
    for cpi in range(11):
        c0, c1 = 2 * cpi, 2 * cpi + 1
        blk = np.concatenate([w_up[:, c0 * 128:(c0 + 1) * 128], w_up[:, 2816 + c0 * 128:2816 + (c0 + 1) * 128],
                              w_up[:, c1 * 128:(c1 + 1) * 128], w_up[:, 2816 + c1 * 128:2816 + (c1 + 1) * 128]], axis=1)
        wts[12 + cpi] = _kc(blk)
    wd = w_down.reshape(22, 128, 1024)
    for half in range(2):
        for m in range(3):
            t = np.zeros((128, 8, 512), f)
            for cc in range(8):
                c = 8 * m + cc
                if c < 22:
                    t[:, cc, :] = wd[c, :, half * 512:(half + 1) * 512]
            wts[23 + 3 * half + m] = t.reshape(128, 4096)
    ukv = w_ukv.reshape(128, 8, 256)
    WukT = np.ascontiguousarray(ukv[:, :, 0:128].transpose(2, 1, 0))
    Wuv = np.ascontiguousarray(ukv[:, :, 128:256])
    spWT = np.ascontiguousarray(np.asarray(inp["a_spatial_w"], f)[0].transpose(2, 0, 1))
    wsm = np.concatenate([WukT.reshape(128, 1024), Wuv.reshape(128, 1024), spWT.reshape(128, 1024)], axis=1)
    cols = np.zeros((128, NCOL), f)
    cols[:, C_EPS] = EPS
    invf = (1.0 / (10000.0 ** (np.arange(0, 64, 2, dtype=np.float32) / 64.0))).astype(f)
    p = np.arange(128)
    cols[:, C_INVF] = invf[p % 32]
    cols[:, C_SGN] = np.where((p % 64) < 32, -1.0, 1.0)
    cols[:, C_G1:C_G1 + 8] = np.asarray(inp["mix_norm"], f)[0].reshape(8, 128).T
    cols[:, C_G2:C_G2 + 8] = np.asarray(inp["ffn_norm"], f)[0].reshape(8, 128).T
    cols[:, C_QG:C_QG + 2] = np.asarray(inp["q_a_norm"], f)[0].reshape(2, 128).T
    cols[:, C_KG] = np.asarray(inp["kv_a_norm"], f)[0]
    cols[:, C_CB:C_CB + 44] = np.asarray(inp["conv_b"], f)[0].reshape(44, 128).T
    cols[:, C_CW:C_CW + 132] = np.asarray(inp["conv_w"], f)[0].reshape(3, 44, 128).transpose(2, 1, 0).reshape(128, 132)
    rows = np.stack([np.asarray(inp["a_v_norm_g"], f)[0], np.asarray(inp["a_v_norm_b"], f)[0],
                     np.asarray(inp["final_norm"], f)], axis=0)
    bsr = np.asarray(inp["a_spatial_b"], f)[0].reshape(1, 1024)
    return dict(wts=wts, wsm=np.ascontiguousarray(wsm), cols=cols, rows=np.ascontiguousarray(rows), bsr=np.ascontiguousarray(bsr))


_CACHE = {}


def kernel(**inputs):
    x = np.asarray(inputs["x"], np.float32)
    pos = np.asarray(inputs["positions"], np.int32)
    w = prep_weights(inputs)
    if "nc" not in _CACHE:
        _CACHE["nc"] = build_program()[0]
    nc = _CACHE["nc"]
    n = 8
    in_maps = []
    for c in range(n):
        m = dict(w)
        m["x"] = np.ascontiguousarray(x[2 * c:2 * c + 2])
        m["pos"] = np.ascontiguousarray(pos[2 * c:2 * c + 2])
        in_maps.append(m)
    res = run_bass_kernel_spmd(nc, in_maps, core_ids=list(range(n)))
    return np.concatenate([np.asarray(r["out"], np.float32) for r in res.results], axis=0)
```
